# Optimizing a Trainium2 kernel written in Bass

```python
import jax, jax.numpy as jnp
from jax import lax
import numpy as np

D_MODEL = 2048
BATCH = 4
SEQ = 4096
DEPTH = 2

CHUNK = 64
N_META = 16
D_LRU = D_MODEL // 2
D_CONV = D_MODEL // 2
D_MIX = D_LRU + D_CONV
LRU_HEADS = 16
LRU_HEAD_DIM = D_LRU // LRU_HEADS
CONV_GROUPS = 16
LRU_CONV_W = 4
SHORT_CONV_W = 3
LRU_C = 8.0
RMS_EPS = 1e-6
SPLIT_SIZES = (D_LRU, D_LRU, D_CONV, D_CONV, D_CONV, D_CONV)
D_IN = sum(SPLIT_SIZES)
SPLIT_IDX = tuple(int(v) for v in np.cumsum(SPLIT_SIZES)[:-1])

kernel_name = "hymba_rglru_shortconv_trunk"


def rmsnorm(x, g):
    xf = x.astype(jnp.float32)
    y = xf * lax.rsqrt(jnp.mean(xf * xf, axis=-1, keepdims=True) + RMS_EPS)
    return (y * g.astype(jnp.float32)).astype(x.dtype)


def causal_depthwise_conv(x, w):
    k, c = w.shape
    return lax.conv_general_dilated(
        x, w[:, None, :].astype(x.dtype), window_strides=(1,),
        padding=((k - 1, 0),), dimension_numbers=("NWC", "WIO", "NWC"),
        feature_group_count=c)


def rg_lru(x, wr, br, wi, bi, lam):
    bsz, length, _ = x.shape
    xf = x.astype(jnp.float32)
    xh = xf.reshape(bsz, length, LRU_HEADS, LRU_HEAD_DIM)
    r = jax.nn.sigmoid(jnp.einsum("blhi,hij->blhj", xh, wr.astype(jnp.float32))
                       .reshape(bsz, length, D_LRU) + br.astype(jnp.float32))
    i = jax.nn.sigmoid(jnp.einsum("blhi,hij->blhj", xh, wi.astype(jnp.float32))
                       .reshape(bsz, length, D_LRU) + bi.astype(jnp.float32))
    log_a = -LRU_C * r * jax.nn.softplus(-lam.astype(jnp.float32))
    a = jnp.exp(log_a)
    b = jnp.sqrt(-jnp.expm1(2.0 * log_a)) * (i * xf)

    def combine(left, right):
        a1, b1 = left
        a2, b2 = right
        return a1 * a2, a2 * b1 + b2

    _, h = lax.associative_scan(combine, (a, b), axis=1)
    return h.astype(x.dtype)


def hybrid_layer(x, norm_g, w_in, conv_a_w, conv_a_b, lru_wr, lru_br, lru_wi,
                 lru_bi, lru_lambda, conv_b_w, w_out):
    h = rmsnorm(x, norm_g)
    u = jnp.einsum("bld,de->ble", h, w_in.astype(h.dtype))
    xa, ga, gate_b, gate_c, xb, gb = jnp.split(u, SPLIT_IDX, axis=-1)
    xa = causal_depthwise_conv(xa, conv_a_w) + conv_a_b.astype(xa.dtype)
    ya = rg_lru(xa, lru_wr, lru_br, lru_wi, lru_bi, lru_lambda) * jax.nn.silu(ga)
    yb = gate_b * causal_depthwise_conv(gate_c * xb, conv_b_w) * jax.nn.silu(gb)
    y = jnp.concatenate([ya, yb], axis=-1)
    return x + jnp.einsum("ble,ed->bld", y, w_out.astype(y.dtype))


def setup_inputs(seed: int = 0) -> dict:
    key = jax.random.key(seed)
    ks = jax.random.split(key, 16)
    f32 = jnp.float32
    x = jax.random.normal(ks[0], (BATCH, SEQ, D_MODEL), f32)
    meta = jax.random.normal(ks[1], (N_META, D_MODEL), f32)
    norm_g = 1.0 + 0.01 * jax.random.normal(ks[2], (DEPTH, D_MODEL), f32)
    w_in = jax.random.normal(ks[3], (DEPTH, D_MODEL, D_IN), f32) * D_MODEL ** -0.5
    conv_a_w = jax.random.normal(ks[4], (DEPTH, LRU_CONV_W, D_LRU), f32) * LRU_CONV_W ** -0.5
    conv_a_b = 0.01 * jax.random.normal(ks[5], (DEPTH, D_LRU), f32)
    lru_wr = jax.random.normal(ks[6], (DEPTH, LRU_HEADS, LRU_HEAD_DIM, LRU_HEAD_DIM), f32) * LRU_HEAD_DIM ** -0.5
    lru_br = 0.01 * jax.random.normal(ks[7], (DEPTH, D_LRU), f32)
    lru_wi = jax.random.normal(ks[8], (DEPTH, LRU_HEADS, LRU_HEAD_DIM, LRU_HEAD_DIM), f32) * LRU_HEAD_DIM ** -0.5
    lru_bi = 0.01 * jax.random.normal(ks[9], (DEPTH, D_LRU), f32)
    a_c = jax.random.uniform(ks[10], (DEPTH, D_LRU), f32, 0.9, 0.999)
    a0 = a_c ** (1.0 / LRU_C)
    lru_lambda = jnp.log(a0) - jnp.log1p(-a0)
    conv_b_w = jax.random.normal(ks[11], (DEPTH, SHORT_CONV_W, D_CONV), f32) * SHORT_CONV_W ** -0.5
    w_out = jax.random.normal(ks[12], (DEPTH, D_MIX, D_MODEL), f32) * D_MIX ** -0.5
    final_g = 1.0 + 0.01 * jax.random.normal(ks[13], (D_MODEL,), f32)
    return {"x": x, "meta": meta, "norm_g": norm_g, "w_in": w_in,
            "conv_a_w": conv_a_w, "conv_a_b": conv_a_b, "lru_wr": lru_wr,
            "lru_br": lru_br, "lru_wi": lru_wi, "lru_bi": lru_bi,
            "lru_lambda": lru_lambda, "conv_b_w": conv_b_w, "w_out": w_out,
            "final_g": final_g}


def reference(x, meta, norm_g, w_in, conv_a_w, conv_a_b, lru_wr, lru_br, lru_wi,
              lru_bi, lru_lambda, conv_b_w, w_out, final_g):
    bsz = x.shape[0]
    m = jnp.broadcast_to(meta.astype(x.dtype)[None], (bsz, N_META, D_MODEL))
    h = jnp.concatenate([m, x], axis=1)
    for layer in range(DEPTH):
        h = hybrid_layer(h, norm_g[layer], w_in[layer], conv_a_w[layer],
                         conv_a_b[layer], lru_wr[layer], lru_br[layer],
                         lru_wi[layer], lru_bi[layer], lru_lambda[layer],
                         conv_b_w[layer], w_out[layer])
    return rmsnorm(h[:, N_META:], final_g)
```

```python
import numpy as np
from contextlib import ExitStack

import concourse.bass as bass
import concourse.mybir as mybir
from concourse.bass_utils import run_bass_kernel_spmd

F32 = mybir.dt.float32
BF16 = mybir.dt.bfloat16
AF = mybir.ActivationFunctionType
ALU = mybir.AluOpType

D = 2048
KT = 16
NL = 2
SEQ = 4096
NMETA = 16
NPRE = 6
OWN = 2056
T = NPRE + OWN
SPLIT = OWN - NMETA
GROUPS = [(0, 413), (413, 413), (826, 412), (1238, 412), (1650, 412)]
NG = len(GROUPS)
GW = 416
RING = 6
STREAM0 = True
TAILD = 4
P2_PASSES = [[0, 1], [2, 3], [4]]
EPS = 1e-6

PL = 104
def _pc(l, off):
    return l * PL + off
P_G, P_CAW, P_CAB, P_BR, P_BI, P_LAM, P_CBW = 0, 16, 48, 56, 64, 72, 80
P_FG = NL * PL
P_FLAG = P_FG + 16
NPRM = P_FLAG + 1


SELF_SYNC = ("act", "dve")


class Tok:
    __slots__ = ("sem", "val", "key")

    def __init__(self, sem, val, key):
        self.sem, self.val, self.key = sem, val, key


class Buf:
    def __init__(self, name):
        self.name = name
        self.w = {}
        self.r = {}
        self.alias = []


class Eng:
    def __init__(self, name, sem):
        self.name, self.sem, self.count = name, sem, 0
        self.waited = {}
        self.prog = []


class DSem:
    def __init__(self, sem, key):
        self.sem, self.key, self.count = sem, key, 0


class Sched:
    def __init__(self):
        self.engs = {}

    def _collect(self, eng, reads, writes):
        need = {}

        def add(d):
            for key, tok in d.items():
                if key == eng.name and eng.name not in SELF_SYNC:
                    continue
                if key not in need or need[key].val < tok.val:
                    need[key] = tok

        for b in reads:
            for bb in [b] + b.alias:
                add(bb.w)
        for b in writes:
            for bb in [b] + b.alias:
                add(bb.w)
                add(bb.r)
        out = []
        for key, tok in need.items():
            if eng.waited.get(key, 0) >= tok.val:
                continue
            eng.waited[key] = tok.val
            out.append(tok)
        return out

    def _update(self, tok, reads, writes):
        for b in reads:
            old = b.r.get(tok.key)
            if old is None or old.val < tok.val:
                b.r[tok.key] = tok
        for b in writes:
            b.w = {tok.key: tok}
            b.r = {}

    def op(self, eng, fn, reads=(), writes=()):
        waits = self._collect(eng, reads, writes)
        eng.count += 1
        tok = Tok(eng.sem, eng.count, eng.name)

        def run(e, waits=waits, fn=fn, sem=eng.sem):
            for w in waits:
                e.wait_ge(w.sem, w.val)
            ins = fn(e)
            ins.then_inc(sem, 1)

        eng.prog.append(run)
        self._update(tok, reads, writes)
        return tok

    def dma(self, eng, dsem, fn, reads=(), writes=()):
        waits = self._collect(eng, reads, writes)
        dsem.count += 16
        tok = Tok(dsem.sem, dsem.count, dsem.key)

        def run(e, waits=waits, fn=fn, sem=dsem.sem):
            for w in waits:
                e.wait_ge(w.sem, w.val)
            fn(e).then_inc(sem, 16)

        eng.prog.append(run)
        self._update(tok, reads, writes)
        return tok

    def coll(self, eng, dsem, fn, reads=(), writes=()):
        waits = self._collect(eng, reads, writes)
        dsem.count += 1
        tok = Tok(dsem.sem, dsem.count, dsem.key)

        def run(e, waits=waits, fn=fn, sem=dsem.sem):
            for w in waits:
                e.wait_ge(w.sem, w.val)
            fn(e).then_inc(sem)

        eng.prog.append(run)
        self._update(tok, reads, writes)
        return tok

    def final_wait(self, eng, bufs):
        waits = self._collect(eng, [], bufs)

        def run(e, waits=waits):
            for w in waits:
                e.wait_ge(w.sem, w.val)

        eng.prog.append(run)


def build_program(dbg=False):
    nc = bass.Bass("TRN2", target_bir_lowering=False)
    xT = nc.dram_tensor("xT", [D, T], F32, kind="ExternalInput").ap()
    w_in = nc.dram_tensor("w_in", [NL * 48 * 128, KT * 128], F32, kind="ExternalInput").ap()
    w_out = nc.dram_tensor("w_out", [NL * 16 * 128, KT * 128], F32, kind="ExternalInput").ap()
    wg = nc.dram_tensor("wg", [NL * 128, 8 * 2 * 128], F32, kind="ExternalInput").ap()
    prm = nc.dram_tensor("prm", [128, NPRM], F32, kind="ExternalInput").ap()
    tmask = nc.dram_tensor("tmask", [128, 8], F32, kind="ExternalInput").ap()
    outT = nc.dram_tensor("outT", [128, KT * T], F32, kind="ExternalOutput").ap()
    spill = []
    for l in range(NL):
        if dbg:
            spill.append(nc.dram_tensor(f"spill{l}", [D, T], F32, kind="ExternalOutput").ap())
        else:
            spill.append(nc.dram_tensor(f"spill{l}", [D, T], F32).ap())
    if dbg:
        dbg_drv = nc.dram_tensor("dbg_drv", [128, NL * 32], F32, kind="ExternalOutput").ap()
        dbg_ab = nc.dram_tensor("dbg_ab", [128, 3 * T], F32, kind="ExternalOutput").ap()
        dbg_st = nc.dram_tensor("dbg_st", [128, 6], F32, kind="ExternalOutput").ap()
    bnc_in = [[nc.dram_tensor(f"bi_{l}_{s}", [128, 1], F32) for s in range(8)] for l in range(NL)]
    bnc_out = [[nc.dram_tensor(f"bo_{l}_{s}", [256, 1], F32) for s in range(8)] for l in range(NL)]

    S = Sched()
    with ExitStack() as es:
        def sb(name, shape, dt):
            return es.enter_context(nc.sbuf_tensor(name, shape, dt))

        def sem(name):
            return es.enter_context(nc.semaphore(name))

        bufA = sb("bufA", [128, KT * T], BF16)
        bufB = sb("bufB", [128, KT * T], BF16)
        LRUW = max(3 * T, KT * GW)
        lru_big = sb("lru_big", [128, LRUW], F32)
        a_full = lru_big[:, 0:T]
        b_full = lru_big[:, T:2 * T]
        t1_full = lru_big[:, 2 * T:3 * T]
        ring = [sb(f"ring{i}", [128, KT * 128], BF16) for i in range(RING)]
        NTMP = 9
        tmp = [sb(f"tmp{i}", [128, GW], F32) for i in range(NTMP)]
        halo = [sb(f"halo{i}", [128, GW + 4], F32) for i in range(4)]
        wgs = sb("wgs", [128, 8 * 2 * 128], BF16)
        prm_sb = sb("prm_sb", [128, NPRM], F32)
        drv = sb("drv", [128, NL * 32], F32)
        sc = sb("scr", [128, 64], F32)
        tm_sb = sb("tm_sb", [128, 8], F32)
        ones = sb("ones", [128, 128], BF16)
        epsb = sb("epsb", [128, 1], F32)
        stx = sb("stx", [128, 2], F32)
        gath = sb("gath", [128, 2], F32)
        init = sb("init", [128, 2], F32)
        banks = [es.enter_context(nc.psum_tensor(f"bank{i}", [128, 512], F32)) for i in range(8)]

        E = {}
        for nm in ["pe", "act", "dve", "pool", "sp"]:
            E[nm] = Eng(nm, sem("s_" + nm))
        pe, act, dve, pool, sp = E["pe"], E["act"], E["dve"], E["pool"], E["sp"]

        def dsem(name):
            return DSem(sem("d_" + name), "d_" + name)

        ring_ds = [dsem(f"ring{i}") for i in range(RING)]
        tmp_ds = [dsem(f"tmp{i}") for i in range(NTMP)]
        stage_ds = [dsem("stage0"), dsem("stage1"), dsem("stage2"), dsem("stage3"), dsem("stage4")]
        misc_ds = dsem("misc")
        st_ds = dsem("stx")
        ga_ds = dsem("gath")
        cc_ds = dsem("cc")

        B_ring = [Buf(f"ring{i}") for i in range(RING)]
        B_tmp = [Buf(f"tmp{i}") for i in range(NTMP)]
        B_halo = [Buf(f"halo{i}") for i in range(4)]
        B_bank = [Buf(f"bank{i}") for i in range(8)]
        B_hn = [Buf(f"hn{g}") for g in range(NG)]
        B_y = [[Buf(f"y{k}_{g}") for g in range(NG)] for k in range(KT)]
        B_stage = [Buf("stage0"), Buf("stage1"), Buf("stage2"), Buf("stage3"), Buf("stage4")]
        for si, ks in enumerate([range(2, 9), range(9, 16)]):
            for k in ks:
                for g in range(NG):
                    B_stage[si].alias.append(B_y[k][g])
                    B_y[k][g].alias.append(B_stage[si])
        B_a = [Buf(f"a{g}") for g in range(NG)]
        B_b = [Buf(f"b{g}") for g in range(NG)]
        B_t1 = [Buf(f"t1{g}") for g in range(NG)]
        B_wgs = Buf("wgs")
        B_prm = Buf("prm")
        B_drv = Buf("drv")
        B_const = Buf("const")
        B_stx = [Buf("stx0"), Buf("stx1")]
        B_gath = [Buf("gath0"), Buf("gath1")]
        B_init = [Buf("init0"), Buf("init1")]
        B_spill = [[[Buf(f"sp{l}_{d}_{g}") for g in range(NG)] for d in range(KT)] for l in range(NL)]
        B_out = [Buf(f"outT{g}") for g in range(NG)]
        B_bin = [[Buf(f"bin{l}_{s}") for s in range(8)] for l in range(NL)]
        B_bout = [[Buf(f"bout{l}_{s}") for s in range(8)] for l in range(NL)]

        for g in range(NG):
            for bb in (B_a[g], B_b[g], B_t1[g]):
                B_stage[2].alias.append(bb)
                bb.alias.append(B_stage[2])
            for si in (3, 4):
                B_stage[si].alias.append(B_hn[g])
                B_hn[g].alias.append(B_stage[si])
        SW = KT * GW
        stage_ap = [bufB[:, 2 * T: 9 * T].bitcast(F32), bufB[:, 9 * T: 16 * T].bitcast(F32),
                    lru_big[:, 0:SW],
                    bufA[:, 0: 2 * SW].bitcast(F32), bufA[:, 2 * SW: 4 * SW].bitcast(F32)]

        def stage_k(si, k, n):
            return stage_ap[si][:, k * GW: k * GW + n]

        rot = {"tmp": 0, "bank": 0, "ring": 0}

        def new_tmp():
            i = rot["tmp"]
            rot["tmp"] = (i + 1) % NTMP
            return i

        bank_free = list(range(8))

        def new_bank():
            assert bank_free, "no free PSUM bank at this point of the schedule"
            return bank_free.pop(0)

        def rel_bank(*bs):
            for b in bs:
                assert b not in bank_free
                bank_free.append(b)

        def take_bank(b):
            assert b in bank_free, f"PSUM bank {b} still live"
            bank_free.remove(b)
            return b

        def P(col, n=1):
            return prm_sb[:, col: col + n]

        wunits = []
        for l in range(NL):
            for u in range(48):
                r0 = (l * 48 + u) * 128
                wunits.append(w_in[r0: r0 + 128, :])
            for d in range(16):
                r0 = (l * 16 + d) * 128
                wunits.append(w_out[r0: r0 + 128, :])
        wstate = {"issued": 0}
        unit_slot = {}

        def prefetch(upto, extra_reads=()):
            upto = min(upto, len(wunits))
            while wstate["issued"] < upto:
                n = wstate["issued"]
                slot = n % RING
                unit_slot[n] = slot
                src = wunits[n]
                S.dma(pool, ring_ds[slot],
                      lambda e, slot=slot, src=src: e.dma_start(out=ring[slot][:, :], in_=src),
                      reads=list(extra_reads), writes=[B_ring[slot]])
                wstate["issued"] += 1

        wpos = {"n": 0}

        def take_units(n):
            base = wpos["n"]
            wpos["n"] += n
            prefetch(base + n)
            return [unit_slot[base + i] for i in range(n)]

        S.dma(sp, misc_ds, lambda e: e.dma_start(out=prm_sb[:, :], in_=prm), writes=[B_prm])
        S.dma(sp, misc_ds, lambda e: e.dma_start(out=tm_sb[:, :], in_=tmask), writes=[B_prm])
        S.op(dve, lambda e: e.memset(epsb[:, :], EPS), writes=[B_const])
        S.op(dve, lambda e: e.memset(ones[:, :], 1.0), writes=[B_const])
        for i in range(4):
            S.op(dve, lambda e, i=i: e.memset(halo[i][:, :], 0.0), writes=[B_halo[i]])

        def derive():
            for l in range(NL):
                derive_layer(l)

        def derive_layer(l):
            if True:
                lam = P(_pc(l, P_LAM), 8)
                o = l * 32
                ee, uu, dd, lnu, rr, zz = (sc[:, 0:8], sc[:, 8:16], sc[:, 16:24], sc[:, 24:32],
                                           sc[:, 32:40], sc[:, 40:48])
                S.op(act, lambda e: e.activation(out=ee, in_=lam, func=AF.Exp, scale=-1.0),
                     reads=[B_prm], writes=[B_drv])
                S.op(dve, lambda e: e.tensor_scalar(out=uu, in0=ee, scalar1=1.0, scalar2=None, op0=ALU.add),
                     reads=[B_drv], writes=[B_drv])
                S.op(dve, lambda e: e.tensor_scalar(out=dd, in0=uu, scalar1=-1.0, scalar2=None, op0=ALU.add),
                     writes=[B_drv])
                S.op(dve, lambda e: e.tensor_scalar(out=zz, in0=dd, scalar1=0.0, scalar2=None, op0=ALU.is_equal),
                     writes=[B_drv])
                S.op(dve, lambda e: e.tensor_scalar(out=dd, in0=dd, scalar1=1e-30, scalar2=None, op0=ALU.max),
                     writes=[B_drv])
                S.op(dve, lambda e: e.reciprocal(out=rr, in_=dd), writes=[B_drv])
                S.op(act, lambda e: e.activation(out=lnu, in_=uu, func=AF.Ln), reads=[B_drv], writes=[B_drv])
                S.op(dve, lambda e: e.tensor_tensor(out=rr, in0=rr, in1=lnu, op=ALU.mult),
                     reads=[B_drv], writes=[B_drv])
                S.op(dve, lambda e: e.tensor_tensor(out=rr, in0=rr, in1=zz, op=ALU.add), writes=[B_drv])
                S.op(dve, lambda e: e.tensor_tensor(out=rr, in0=rr, in1=ee, op=ALU.mult), writes=[B_drv])
                S.op(dve, lambda e, o=o: e.tensor_scalar(out=drv[:, o + 16: o + 24], in0=rr, scalar1=-8.0,
                                                         scalar2=None, op0=ALU.mult), writes=[B_drv])
                S.op(dve, lambda e, o=o: e.tensor_scalar(out=drv[:, o + 24: o + 32], in0=rr, scalar1=-4.0,
                                                         scalar2=None, op0=ALU.mult), writes=[B_drv])
                S.op(dve, lambda e, o=o, l=l: e.tensor_scalar(out=drv[:, o: o + 8], in0=P(_pc(l, P_BR), 8),
                                                              scalar1=0.5, scalar2=None, op0=ALU.mult),
                     writes=[B_drv])
                S.op(dve, lambda e, o=o, l=l: e.tensor_scalar(out=drv[:, o + 8: o + 16], in0=P(_pc(l, P_BI), 8),
                                                              scalar1=0.5, scalar2=None, op0=ALU.mult),
                     writes=[B_drv])

        derive()

        def DV(l, which, s):
            o = l * 32 + which * 8 + s
            return drv[:, o: o + 1]

        def phase0_load(src_ap, src_bufs, g, si):
            off, n = GROUPS[g]
            S.dma(sp, stage_ds[si],
                  lambda e: e.dma_start(
                      out=stage_ap[si][:, 0: KT * GW].rearrange("p (k w) -> p k w", k=KT)[:, :, 0:n],
                      in_=src_ap.rearrange("(k p) t -> p k t", p=128)[:, :, off: off + n]),
                  reads=src_bufs, writes=[B_stage[si]])

        def phase0_group(src_ap, src_bufs, g, si, gcol0, final):
            phase0_load(src_ap, src_bufs, g, si)
            phase0_compute(g, si, gcol0, final)

        def phase0_compute(g, si, gcol0, final):
            off, n = GROUPS[g]
            bk = new_bank()
            for k in range(KT):
                ti = new_tmp()
                S.op(act, lambda e, k=k, ti=ti: e.activation(out=tmp[ti][:, 0:n].bitcast(BF16)[:, 0:n],
                                                             in_=stage_k(si, k, n), func=AF.Square),
                     reads=[B_stage[si]], writes=[B_tmp[ti]])
                S.op(pe, lambda e, k=k, ti=ti: e.matmul(banks[bk][:, 0:n], lhsT=ones[:, :],
                                                        rhs=tmp[ti][:, 0:n].bitcast(BF16)[:, 0:n],
                                                        start=(k == 0), stop=(k == KT - 1)),
                     reads=[B_tmp[ti], B_const], writes=[B_bank[bk]])
            ti = new_tmp()
            S.op(act, lambda e: e.activation(out=tmp[ti][:, 0:n], in_=banks[bk][:, 0:n], func=AF.Sqrt,
                                             bias=epsb[:, 0:1], scale=1.0 / D),
                 reads=[B_bank[bk], B_const], writes=[B_tmp[ti]])
            S.op(dve, lambda e: e.reciprocal(out=banks[bk][:, 0:n], in_=tmp[ti][:, 0:n]),
                 reads=[B_tmp[ti]], writes=[B_bank[bk]])
            for k in range(KT):
                if not final:
                    S.op(dve, lambda e, k=k: e.scalar_tensor_tensor(
                        out=bufA[:, k * T + off: k * T + off + n], in0=stage_k(si, k, n),
                        scalar=P(gcol0 + k), in1=banks[bk][:, 0:n], op0=ALU.mult, op1=ALU.mult),
                        reads=[B_stage[si], B_bank[bk], B_prm], writes=[B_hn[g]])
                else:
                    S.op(dve, lambda e, k=k: e.scalar_tensor_tensor(
                        out=stage_k(si, k, n), in0=stage_k(si, k, n),
                        scalar=P(gcol0 + k), in1=banks[bk][:, 0:n], op0=ALU.mult, op1=ALU.mult),
                        reads=[B_bank[bk], B_prm], writes=[B_stage[si]])
            if final:
                S.dma(pool, stage_ds[si],
                      lambda e: e.dma_start(
                          out=outT.rearrange("(k p) t -> p k t", p=128)[:, :, off: off + n],
                          in_=stage_ap[si][:, 0: KT * GW].rearrange("p (k w) -> p k w", k=KT)[:, :, 0:n]),
                      reads=[B_stage[si]], writes=[B_out[g]])
            rel_bank(bk)

        def mm_group(bk, unit, g):
            off, n = GROUPS[g]

            def fn(e):
                ins = None
                for k in range(KT):
                    ins = e.matmul(banks[bk][:, 0:n], lhsT=ring[unit][:, k * 128:(k + 1) * 128],
                                   rhs=bufA[:, k * T + off: k * T + off + n],
                                   start=(k == 0), stop=(k == KT - 1))
                return ins
            S.op(pe, fn, reads=[B_ring[unit], B_hn[g]], writes=[B_bank[bk]])

        def lru_main(l, s, g, units):
            bks = [new_bank(), new_bank()]
            mm_group(bks[0], units[0], g)
            mm_group(bks[1], units[1], g)
            return bks

        def xcb_view(off, n):
            return lru_big[:, off:off + n].bitcast(BF16)[:, 0:n]

        def lru_stage1(l, s, g, bks):
            off, n = GROUPS[g]
            hb = halo[g % 2]
            hbn = halo[(g + 1) % 2]
            Bh, Bhn = B_halo[g % 2], B_halo[(g + 1) % 2]
            xa_ps, ga_ps = banks[bks[0]], banks[bks[1]]
            Bxa, Bga = B_bank[bks[0]], B_bank[bks[1]]
            cw = _pc(l, P_CAW) + s * 4
            if g == 0:
                S.op(dve, lambda e: e.memset(hb[:, 0:3], 0.0), writes=[Bh])
            S.op(act, lambda e: e.activation(out=hb[:, 3:3 + n], in_=xa_ps[:, 0:n], func=AF.Identity),
                 reads=[Bxa], writes=[Bh])
            S.op(act, lambda e: e.activation(out=xa_ps[:, 0:n], in_=xa_ps[:, 0:n], func=AF.Identity,
                                             scale=P(cw + 3), bias=P(_pc(l, P_CAB) + s)),
                 reads=[B_prm], writes=[Bxa])
            thg = new_tmp()
            S.op(act, lambda e: e.activation(out=tmp[thg][:, 0:n], in_=ga_ps[:, 0:n], func=AF.Tanh, scale=0.5),
                 reads=[Bga], writes=[B_tmp[thg]])
            if g + 1 < NG:
                S.op(act, lambda e: e.activation(out=hbn[:, 0:3], in_=hb[:, n:n + 3], func=AF.Identity),
                     reads=[Bh], writes=[Bhn])
            for tap in (2, 1):
                S.op(dve, lambda e, tap=tap: e.scalar_tensor_tensor(
                    out=xa_ps[:, 0:n], in0=hb[:, tap:tap + n], scalar=P(cw + tap), in1=xa_ps[:, 0:n],
                    op0=ALU.mult, op1=ALU.add),
                    reads=[Bh, B_prm], writes=[Bxa])
            S.op(dve, lambda e: e.scalar_tensor_tensor(
                out=b_full[:, off:off + n], in0=hb[:, 0:n], scalar=P(cw + 0), in1=xa_ps[:, 0:n],
                op0=ALU.mult, op1=ALU.add),
                reads=[Bh, Bxa, B_prm], writes=[B_b[g]])
            S.op(dve, lambda e: e.scalar_tensor_tensor(
                out=t1_full[:, off:off + n], in0=tmp[thg][:, 0:n], scalar=1.0, in1=ga_ps[:, 0:n],
                op0=ALU.add, op1=ALU.mult),
                reads=[B_tmp[thg], Bga], writes=[B_t1[g]])
            S.op(act, lambda e: e.activation(out=xcb_view(off, n), in_=b_full[:, off:off + n],
                                             func=AF.Identity),
                 reads=[B_b[g]], writes=[B_a[g]])
            rel_bank(*bks)

        def lru_gates(l, s, g):
            off, n = GROUPS[g]
            gb2 = [new_bank(), new_bank()]
            for gi in range(2):
                S.op(pe, lambda e, gi=gi: e.matmul(
                    banks[gb2[gi]][:, 0:n], lhsT=wgs[:, (s * 2 + gi) * 128:(s * 2 + gi + 1) * 128],
                    rhs=xcb_view(off, n), start=True, stop=True),
                    reads=[B_wgs, B_a[g]], writes=[B_bank[gb2[gi]]])
            return gb2

        def lru_stage2(l, s, g, gb2):
            off, n = GROUPS[g]
            r_ps, i_ps = banks[gb2[0]], banks[gb2[1]]
            Br, Bi = B_bank[gb2[0]], B_bank[gb2[1]]
            t_r, t_i = new_tmp(), new_tmp()
            S.op(act, lambda e: e.activation(out=tmp[t_r][:, 0:n], in_=r_ps[:, 0:n], func=AF.Tanh,
                                             scale=0.5, bias=DV(l, 0, s)),
                 reads=[Br, B_drv], writes=[B_tmp[t_r]])
            S.op(act, lambda e: e.activation(out=tmp[t_i][:, 0:n], in_=i_ps[:, 0:n], func=AF.Tanh,
                                             scale=0.5, bias=DV(l, 1, s)),
                 reads=[Bi, B_drv], writes=[B_tmp[t_i]])
            S.op(act, lambda e: e.activation(out=a_full[:, off:off + n], in_=tmp[t_r][:, 0:n], func=AF.Exp,
                                             scale=DV(l, 3, s), bias=DV(l, 3, s)),
                 reads=[B_tmp[t_r], B_drv], writes=[B_a[g]])
            S.op(act, lambda e: e.activation(out=tmp[t_r][:, 0:n], in_=tmp[t_r][:, 0:n], func=AF.Exp,
                                             scale=DV(l, 2, s), bias=DV(l, 2, s)),
                 reads=[B_drv], writes=[B_tmp[t_r]])
            S.op(act, lambda e: e.activation(out=tmp[t_r][:, 0:n], in_=tmp[t_r][:, 0:n], func=AF.Sqrt,
                                             scale=-0.25, bias=0.25),
                 reads=[], writes=[B_tmp[t_r]])
            S.op(dve, lambda e: e.scalar_tensor_tensor(
                out=tmp[t_i][:, 0:n], in0=tmp[t_i][:, 0:n], scalar=1.0, in1=b_full[:, off:off + n],
                op0=ALU.add, op1=ALU.mult),
                reads=[B_b[g]], writes=[B_tmp[t_i]])
            S.op(dve, lambda e: e.tensor_tensor(out=b_full[:, off:off + n], in0=tmp[t_i][:, 0:n],
                                                in1=tmp[t_r][:, 0:n], op=ALU.mult),
                 reads=[B_tmp[t_i], B_tmp[t_r]], writes=[B_b[g]])
            if g == 0:
                S.op(dve, lambda e: e.tensor_tensor(out=b_full[:, 0:8], in0=b_full[:, 0:8], in1=tm_sb[:, 0:8],
                                                    op=ALU.mult),
                     reads=[B_prm], writes=[B_b[g]])
            rel_bank(*gb2)

        def scan_pass(l, s, ybase, which):
            s0 = 3 * (l + 1)
            par = s % 2
            prev = None
            if which == 2:
                S.op(dve, lambda e: e.tensor_tensor(out=init[:, par:par + 1], in0=gath[:, par:par + 1],
                                                    in1=P(P_FLAG), op=ALU.mult),
                     reads=[B_gath[par], B_prm], writes=[B_init[par]])
            for g in range(NG):
                off, n = GROUPS[g]
                lo = s0 if g == 0 else 0
                scr = new_tmp()
                if g == 0:
                    if which == 2:
                        S.op(dve, lambda e, scr=scr, lo=lo: e.memset(tmp[scr][:, 0:lo], 0.0), writes=[B_tmp[scr]])
                        ini = init[:, par:par + 1]
                        inib = [B_init[par]]
                    else:
                        ini = 0.0
                        inib = []
                else:
                    pn = GROUPS[g - 1][1]
                    ini = tmp[prev][:, pn - 1:pn]
                    inib = [B_tmp[prev]]
                S.op(dve, lambda e, scr=scr, ini=ini, lo=lo, off=off, n=n: e.tensor_tensor_scan(
                    out=tmp[scr][:, lo:n], data0=a_full[:, off + lo:off + n], data1=b_full[:, off + lo:off + n],
                    initial=ini, op0=ALU.mult, op1=ALU.add),
                    reads=[B_a[g], B_b[g]] + inib, writes=[B_tmp[scr]])
                if which == 2:
                    S.op(dve, lambda e, scr=scr, off=off, n=n: e.scalar_tensor_tensor(
                        out=bufB[:, ybase * T + off: ybase * T + off + n], in0=tmp[scr][:, 0:n], scalar=0.5,
                        in1=t1_full[:, off:off + n], op0=ALU.mult, op1=ALU.mult),
                        reads=[B_tmp[scr], B_t1[g]], writes=[B_y[ybase][g]])
                prev = scr
            if which == 1:
                n = GROUPS[NG - 1][1]
                col = n - 4 if l == 0 else n - 1
                S.op(dve, lambda e: e.tensor_copy(out=stx[:, par:par + 1], in_=tmp[prev][:, col:col + 1]),
                     reads=[B_tmp[prev]], writes=[B_stx[par]])
                S.dma(sp, st_ds, lambda e: e.dma_start(out=bnc_in[l][s].ap(), in_=stx[:, par:par + 1]),
                      reads=[B_stx[par]], writes=[B_bin[l][s]])

        def exchange(l, s):
            par = s % 2
            S.coll(pool, cc_ds, lambda e: e.collective_compute(
                "AllGather", ALU.bypass, replica_groups=[[0, 1], [2, 3], [4, 5], [6, 7]],
                ins=[bnc_in[l][s].ap().opt()], outs=[bnc_out[l][s].ap().opt()]),
                reads=[B_bin[l][s]], writes=[B_bout[l][s]])
            S.dma(sp, ga_ds, lambda e: e.dma_start(out=gath[:, par:par + 1], in_=bnc_out[l][s].ap()[0:128, :]),
                  reads=[B_bout[l][s]], writes=[B_gath[par]])

        def conv_main(l, s, g, units):
            bks = [new_bank() for _ in range(4)]
            for j in range(4):
                mm_group(bks[j], units[j], g)
            return bks

        def conv_stage(l, s, g, bks, ybase):
            off, n = GROUPS[g]
            hb = halo[2 + g % 2]
            hbn = halo[2 + (g + 1) % 2]
            Bh, Bhn = B_halo[2 + g % 2], B_halo[2 + (g + 1) % 2]
            gb_ps_, gc_ps, xb_ps, gbk = banks[bks[0]], banks[bks[1]], banks[bks[2]], banks[bks[3]]
            Bgb, Bgc, Bxb, Bgk = B_bank[bks[0]], B_bank[bks[1]], B_bank[bks[2]], B_bank[bks[3]]
            cw = _pc(l, P_CBW) + s * 3
            if g == 0:
                S.op(dve, lambda e: e.memset(hb[:, 0:2], 0.0), writes=[Bh])
            xb = new_tmp()
            S.op(act, lambda e: e.activation(out=tmp[xb][:, 0:n], in_=xb_ps[:, 0:n], func=AF.Identity),
                 reads=[Bxb], writes=[B_tmp[xb]])
            thg = new_tmp()
            S.op(act, lambda e: e.activation(out=tmp[thg][:, 0:n], in_=gbk[:, 0:n], func=AF.Tanh, scale=0.5),
                 reads=[Bgk], writes=[B_tmp[thg]])
            S.op(dve, lambda e: e.tensor_tensor(out=hb[:, 2:2 + n], in0=tmp[xb][:, 0:n], in1=gc_ps[:, 0:n],
                                                op=ALU.mult),
                 reads=[B_tmp[xb], Bgc], writes=[Bh])
            S.op(act, lambda e: e.activation(out=gc_ps[:, 0:n], in_=hb[:, 2:2 + n], func=AF.Identity,
                                             scale=P(cw + 2)),
                 reads=[Bh, B_prm], writes=[Bgc])
            if g + 1 < NG:
                S.op(act, lambda e: e.activation(out=hbn[:, 0:2], in_=hb[:, n:n + 2], func=AF.Identity),
                     reads=[Bh], writes=[Bhn])
            S.op(dve, lambda e: e.scalar_tensor_tensor(
                out=gc_ps[:, 0:n], in0=hb[:, 1:1 + n], scalar=P(cw + 1), in1=gc_ps[:, 0:n],
                op0=ALU.mult, op1=ALU.add),
                reads=[Bh, B_prm], writes=[Bgc])
            q2 = new_tmp()
            S.op(dve, lambda e: e.scalar_tensor_tensor(
                out=tmp[q2][:, 0:n], in0=hb[:, 0:n], scalar=P(cw + 0), in1=gc_ps[:, 0:n],
                op0=ALU.mult, op1=ALU.add),
                reads=[Bh, Bgc, B_prm], writes=[B_tmp[q2]])
            S.op(dve, lambda e: e.tensor_tensor(out=tmp[q2][:, 0:n], in0=tmp[q2][:, 0:n], in1=gb_ps_[:, 0:n],
                                                op=ALU.mult),
                 reads=[Bgb], writes=[B_tmp[q2]])
            S.op(dve, lambda e: e.scalar_tensor_tensor(
                out=xb_ps[:, 0:n], in0=tmp[thg][:, 0:n], scalar=1.0, in1=gbk[:, 0:n],
                op0=ALU.add, op1=ALU.mult),
                reads=[B_tmp[thg], Bgk], writes=[Bxb])
            S.op(dve, lambda e: e.scalar_tensor_tensor(
                out=bufB[:, ybase * T + off: ybase * T + off + n], in0=tmp[q2][:, 0:n], scalar=0.5,
                in1=xb_ps[:, 0:n], op0=ALU.mult, op1=ALU.mult),
                reads=[B_tmp[q2], Bxb], writes=[B_y[ybase][g]])
            rel_bank(*bks)

        def phase2_item(l, d, g, unit, src_ap, src_buf):
            off, n = GROUPS[g]
            hin = new_tmp()
            S.dma(sp, tmp_ds[hin],
                  lambda e: e.dma_start(out=tmp[hin][:, 0:n], in_=src_ap[d * 128:(d + 1) * 128, off:off + n]),
                  reads=src_buf, writes=[B_tmp[hin]])
            bk = new_bank()

            def fn(e):
                ins = None
                for k in range(KT):
                    ins = e.matmul(banks[bk][:, 0:n], lhsT=ring[unit][:, k * 128:(k + 1) * 128],
                                   rhs=bufB[:, k * T + off: k * T + off + n],
                                   start=(k == 0), stop=(k == KT - 1))
                return ins
            S.op(pe, fn, reads=[B_ring[unit]] + [B_y[k][g] for k in range(KT)], writes=[B_bank[bk]])
            S.op(dve, lambda e: e.tensor_tensor(out=tmp[hin][:, 0:n], in0=tmp[hin][:, 0:n],
                                                in1=banks[bk][:, 0:n], op=ALU.add),
                 reads=[B_bank[bk]], writes=[B_tmp[hin]])
            S.dma(pool, tmp_ds[hin],
                  lambda e: e.dma_start(out=spill[l][d * 128:(d + 1) * 128, off:off + n], in_=tmp[hin][:, 0:n]),
                  reads=[B_tmp[hin]], writes=[B_spill[l][d][g]])

        bufA_f32 = bufA[:, :].bitcast(F32)
        wgs_f32 = wgs[:, :].bitcast(F32)
        NHOLD = KT - TAILD
        B_hold = [[Buf(f"hold{d}_{g}") for g in range(NG)] for d in range(NHOLD)]
        for d in range(NHOLD):
            for g in range(NG):
                if d < 8:
                    others = B_hn
                elif d < 11:
                    others = B_a + B_b + B_t1
                else:
                    others = [B_halo[g]] if g < 4 else [B_wgs]
                for ob in others:
                    B_hold[d][g].alias.append(ob)
                    ob.alias.append(B_hold[d][g])
        B_out2 = []
        out_ds = dsem("outs")

        def hold_ap(d, g, off, n):
            if d < 8:
                return bufA_f32[:, 8 * off + d * n:8 * off + (d + 1) * n]
            if d < 11:
                return lru_big[:, 3 * off + (d - 8) * n:3 * off + (d - 7) * n]
            if g < 4:
                return halo[g][:, 0:n]
            return wgs_f32[:, 0:n]

        def out_ap(g, d0, d1):
            off, n = GROUPS[g]
            return outT[:, KT * off + d0 * n:KT * off + d1 * n]

        def phase2_v2(l, src_ap, src_bufs):
            last = (l == NL - 1)
            SQ = [take_bank(b) for b in (0, 1, 2, 3, 4)]
            DL = [take_bank(b) for b in (5, 6, 7)]
            st = {"dl": 0}
            gnext = P_FG if last else _pc(l + 1, P_G)
            pend = []
            tail_tmps = [[] for _ in range(NG)]
            free_t = list(range(NTMP))
            free_h = []

            def t_alloc():
                if not last:
                    return new_tmp()
                assert free_t, "no free temp"
                return free_t.pop(0)

            def t_free(i):
                if last:
                    free_t.append(i)

            def h_alloc():
                if not last:
                    return (new_tmp(), 0)
                if not free_h:
                    i = t_alloc()
                    free_h.extend([(i, 0), (i, 1)])
                return free_h.pop(0)

            def h_free(x):
                if last:
                    free_h.append(x)

            def sq_ap(x, n):
                i, h = x
                return tmp[i][:, h * (GW // 2):(h + 1) * (GW // 2)].bitcast(BF16)[:, 0:n]

            def flush_one():
                d, g, sq = pend.pop(0)
                off, n = GROUPS[g]
                S.op(pe, lambda e: e.matmul(banks[SQ[g]][:, 0:n], lhsT=ones[:, :],
                                            rhs=sq_ap(sq, n),
                                            start=(d == 0), stop=(d == KT - 1)),
                     reads=[B_tmp[sq[0]], B_const], writes=[B_bank[SQ[g]]])
                h_free(sq)

            def item(d, g, unit):
                off, n = GROUPS[g]
                hin = t_alloc()
                S.dma(sp, tmp_ds[hin],
                      lambda e: e.dma_start(out=tmp[hin][:, 0:n], in_=src_ap[d * 128:(d + 1) * 128, off:off + n]),
                      reads=(src_bufs[g] if l > 0 else []), writes=[B_tmp[hin]])
                bk = DL[st["dl"] % len(DL)]
                st["dl"] += 1

                def fn(e):
                    ins = None
                    for k in range(KT):
                        ins = e.matmul(banks[bk][:, 0:n], lhsT=ring[unit][:, k * 128:(k + 1) * 128],
                                       rhs=bufB[:, k * T + off: k * T + off + n],
                                       start=(k == 0), stop=(k == KT - 1))
                    return ins
                S.op(pe, fn, reads=[B_ring[unit]] + [B_y[k][g] for k in range(KT)], writes=[B_bank[bk]])
                if len(pend) >= 2:
                    flush_one()
                S.op(dve, lambda e: e.tensor_tensor(out=tmp[hin][:, 0:n], in0=tmp[hin][:, 0:n],
                                                    in1=banks[bk][:, 0:n], op=ALU.add),
                     reads=[B_bank[bk]], writes=[B_tmp[hin]])
                sq = h_alloc()
                S.op(act, lambda e: e.activation(out=sq_ap(sq, n), in_=tmp[hin][:, 0:n],
                                                 func=AF.Square),
                     reads=[B_tmp[hin]], writes=[B_tmp[sq[0]]])
                pend.append((d, g, sq))
                if not last:
                    S.dma(pool, tmp_ds[hin],
                          lambda e: e.dma_start(out=spill[l][d * 128:(d + 1) * 128, off:off + n],
                                                in_=tmp[hin][:, 0:n]),
                          reads=[B_tmp[hin]], writes=[B_spill[l][d][g]])
                    S.op(act, lambda e: e.activation(out=bufA[:, d * T + off: d * T + off + n],
                                                     in_=tmp[hin][:, 0:n], func=AF.Identity,
                                                     scale=P(gnext + d)),
                         reads=[B_tmp[hin], B_prm], writes=[B_hn[g]])
                elif d < NHOLD:
                    S.op(act, lambda e: e.activation(out=hold_ap(d, g, off, n), in_=tmp[hin][:, 0:n],
                                                     func=AF.Identity, scale=P(gnext + d)),
                         reads=[B_tmp[hin], B_prm], writes=[B_hold[d][g]])
                    t_free(hin)
                else:
                    S.op(act, lambda e: e.activation(out=tmp[hin][:, 0:n], in_=tmp[hin][:, 0:n],
                                                     func=AF.Identity, scale=P(gnext + d)),
                         reads=[B_prm], writes=[B_tmp[hin]])
                    tail_tmps[g].append((d, hin))

            def finalize(g):
                off, n = GROUPS[g]
                bk = SQ[g]
                ti = t_alloc()
                S.op(act, lambda e: e.activation(out=tmp[ti][:, 0:n], in_=banks[bk][:, 0:n], func=AF.Ln,
                                                 bias=epsb[:, 0:1], scale=1.0 / D),
                     reads=[B_bank[bk], B_const], writes=[B_tmp[ti]])
                S.op(act, lambda e: e.activation(out=banks[bk][:, 0:n], in_=tmp[ti][:, 0:n], func=AF.Exp,
                                                 scale=-0.5),
                     reads=[B_tmp[ti]], writes=[B_bank[bk]])
                t_free(ti)
                if last:
                    for d, hin in tail_tmps[g]:
                        S.op(dve, lambda e, hin=hin: e.tensor_tensor(out=tmp[hin][:, 0:n], in0=tmp[hin][:, 0:n],
                                                                     in1=banks[bk][:, 0:n], op=ALU.mult),
                             reads=[B_bank[bk]], writes=[B_tmp[hin]])
                        bo = Buf(f"o2d_{g}_{d}")
                        B_out2.append(bo)
                        S.dma(pool, tmp_ds[hin],
                              lambda e, d=d, hin=hin: e.dma_start(out=out_ap(g, d, d + 1),
                                                                  in_=tmp[hin][:, 0:n]),
                              reads=[B_tmp[hin]], writes=[bo])
                        t_free(hin)
                    del tail_tmps[g][:]
                if not last:
                    for k in range(KT):
                        S.op(dve, lambda e, k=k: e.tensor_tensor(
                            out=bufA[:, k * T + off: k * T + off + n], in0=bufA[:, k * T + off: k * T + off + n],
                            in1=banks[bk][:, 0:n], op=ALU.mult),
                            reads=[B_bank[bk]], writes=[B_hn[g]])
                    return
                for d in range(NHOLD):
                    S.op(dve, lambda e, d=d: e.tensor_tensor(out=hold_ap(d, g, off, n), in0=hold_ap(d, g, off, n),
                                                             in1=banks[bk][:, 0:n], op=ALU.mult),
                         reads=[B_bank[bk]], writes=[B_hold[d][g]])
                bo = Buf(f"o2a_{g}")
                B_out2.append(bo)
                S.dma(pool, out_ds, lambda e: e.dma_start(
                    out=out_ap(g, 0, 8), in_=bufA_f32[:, 8 * off:8 * off + 8 * n], max_dma_last_dim=4096),
                    reads=[B_hold[d][g] for d in range(0, 8)], writes=[bo])
                bo = Buf(f"o2b_{g}")
                B_out2.append(bo)
                S.dma(pool, out_ds, lambda e: e.dma_start(
                    out=out_ap(g, 8, 11), in_=lru_big[:, 3 * off:3 * off + 3 * n], max_dma_last_dim=4096),
                    reads=[B_hold[d][g] for d in range(8, 11)], writes=[bo])
                bo = Buf(f"o2c_{g}")
                B_out2.append(bo)
                S.dma(pool, out_ds, lambda e: e.dma_start(
                    out=out_ap(g, 11, 12), in_=hold_ap(11, g, off, n)),
                    reads=[B_hold[11][g]], writes=[bo])

            for d in range(KT - TAILD):
                unit = take_units(1)[0]
                for g in range(NG):
                    item(d, g, unit)
                prefetch(wpos["n"] + RING)
            units = take_units(TAILD)
            for g in range(NG):
                for j in range(TAILD):
                    item(KT - TAILD + j, g, units[j])
                    if j == 1 and g > 0:
                        finalize(g - 1)
                        DL.append(SQ[g - 1])
                while pend:
                    flush_one()
            finalize(NG - 1)
            prefetch(wpos["n"] + RING)
            rel_bank(*sorted(set(SQ + DL)))

        def phase0_stream_group(g):
            off, n = GROUPS[g]
            bk = new_bank()
            for d in range(KT):
                hin = new_tmp()
                S.dma(sp, tmp_ds[hin],
                      lambda e, d=d, hin=hin: e.dma_start(out=tmp[hin][:, 0:n],
                                                          in_=xT[d * 128:(d + 1) * 128, off:off + n]),
                      reads=[], writes=[B_tmp[hin]])
                sq = new_tmp()
                S.op(act, lambda e, hin=hin, sq=sq: e.activation(out=tmp[sq][:, 0:n].bitcast(BF16)[:, 0:n],
                                                                 in_=tmp[hin][:, 0:n], func=AF.Square),
                     reads=[B_tmp[hin]], writes=[B_tmp[sq]])
                S.op(pe, lambda e, d=d, sq=sq: e.matmul(banks[bk][:, 0:n], lhsT=ones[:, :],
                                                        rhs=tmp[sq][:, 0:n].bitcast(BF16)[:, 0:n],
                                                        start=(d == 0), stop=(d == KT - 1)),
                     reads=[B_tmp[sq], B_const], writes=[B_bank[bk]])
                S.op(act, lambda e, d=d, hin=hin: e.activation(out=bufA[:, d * T + off: d * T + off + n],
                                                               in_=tmp[hin][:, 0:n], func=AF.Identity,
                                                               scale=P(_pc(0, P_G) + d)),
                     reads=[B_tmp[hin], B_prm], writes=[B_hn[g]])
            ti = new_tmp()
            S.op(act, lambda e: e.activation(out=tmp[ti][:, 0:n], in_=banks[bk][:, 0:n], func=AF.Sqrt,
                                             bias=epsb[:, 0:1], scale=1.0 / D),
                 reads=[B_bank[bk], B_const], writes=[B_tmp[ti]])
            S.op(dve, lambda e: e.reciprocal(out=banks[bk][:, 0:n], in_=tmp[ti][:, 0:n]),
                 reads=[B_tmp[ti]], writes=[B_bank[bk]])
            for k in range(KT):
                S.op(dve, lambda e, k=k: e.tensor_tensor(
                    out=bufA[:, k * T + off: k * T + off + n], in0=bufA[:, k * T + off: k * T + off + n],
                    in1=banks[bk][:, 0:n], op=ALU.mult),
                    reads=[B_bank[bk]], writes=[B_hn[g]])
            rel_bank(bk)

        def first_pair(gcol0):
            uL = take_units(2)
            phase0_load(xT, [], 0, 0)
            prefetch(RING, extra_reads=[B_stage[0]])
            uC = take_units(4)
            phase0_load(xT, [B_stage[0]], 1, 1)
            phase0_compute(0, 0, gcol0, False)
            queue = []
            prev_c = None
            for g in range(NG):
                if prev_c is not None:
                    conv_stage(0, 0, prev_c[0], prev_c[1], 1)
                bL = lru_main(0, 0, g, uL)
                if g == NG - 1:
                    prefetch(wpos["n"] + 2)
                lru_stage1(0, 0, g, bL)
                if g + 1 < NG:
                    phase0_compute(g + 1, (g + 1) % 2, gcol0, False)
                bC = conv_main(0, 0, g, uC)
                prev_c = (g, bC)
                lru_stage2(0, 0, g, lru_gates(0, 0, g))
                if g + 2 < NG:
                    phase0_load(xT, [], g + 2, g % 2)
            prefetch(wpos["n"] + RING)
            scan_pass(0, 0, 0, 1)
            exchange(0, 0)
            conv_stage(0, 0, prev_c[0], prev_c[1], 1)
            scan_pass(0, 0, 0, 2)

        dbg_bufs = []
        prefetch(2)
        for l in range(NL):
            S.dma(pool, misc_ds, lambda e, l=l: e.dma_start(out=wgs[:, :], in_=wg[l * 128:(l + 1) * 128, :]),
                  writes=[B_wgs])
            if l == 0:
                src_ap, src_bufs = xT, [[] for _ in range(NG)]
            else:
                src_ap = spill[l - 1]
                src_bufs = [[B_spill[l - 1][d][g] for d in range(KT)] for g in range(NG)]
            gcol0 = _pc(l, P_G)
            first_slot = 0
            if l == 0:
                first_pair(gcol0)
                first_slot = 2
            pending2 = None
            deferred = []
            for slot in range(first_slot, 16):
                s = slot // 2
                ybase = slot
                if slot % 2 == 0:
                    units = take_units(2)
                    lag = 2
                    queue = []
                    for g in range(NG):
                        bks = lru_main(l, s, g, units)
                        lru_stage1(l, s, g, bks)
                        queue.append(g)
                        if len(queue) > lag:
                            pg = queue.pop(0)
                            lru_stage2(l, s, pg, lru_gates(l, s, pg))
                        if l == 0 and slot == 0 and g + 1 < NG:
                            if STREAM0:
                                phase0_stream_group(g + 1)
                            else:
                                phase0_group(src_ap, src_bufs[g + 1], g + 1, (g + 1) % 2, gcol0, False)
                    deferred = [(l, s, pg) for pg in queue]
                    prefetch(wpos["n"] + RING)
                    pending2 = (l, s, ybase)
                else:
                    units = take_units(4)
                    prev_item = None
                    for g in range(NG):
                        bks = conv_main(l, s, g, units)
                        if prev_item is not None:
                            conv_stage(l, s, prev_item[0], prev_item[1], ybase)
                        prev_item = (g, bks)
                        if deferred:
                            dl, ds_, dg = deferred.pop(0)
                            lru_stage2(dl, ds_, dg, lru_gates(dl, ds_, dg))
                            if not deferred and pending2 is not None:
                                scan_pass(pending2[0], pending2[1], pending2[2], 1)
                                exchange(pending2[0], pending2[1])
                        if g == 4 and pending2 is not None:
                            scan_pass(pending2[0], pending2[1], pending2[2], 2)
                            pending2 = None
                    conv_stage(l, s, prev_item[0], prev_item[1], ybase)
                    prefetch(wpos["n"] + RING)
            phase2_v2(l, src_ap, src_bufs)
        S.final_wait(sp, B_out + B_out2 + ([B_spill[l][d][g] for l in range(NL) for d in range(KT) for g in range(NG)]
                                    if dbg else []) + dbg_bufs)

        block = es.enter_context(nc.Block())

        @block.tensor
        def _(e):
            for f in pe.prog:
                f(e)

        @block.scalar
        def _(e):
            for f in act.prog:
                f(e)

        @block.vector
        def _(e):
            for f in dve.prog:
                f(e)

        @block.gpsimd
        def _(e):
            for f in pool.prog:
                f(e)

        @block.sync
        def _(e):
            for f in sp.prog:
                f(e)
    return nc


def _tile_units(w, col_blocks):
    out = []
    for c0 in col_blocks:
        blk = w[:, c0:c0 + 128]
        out.append(blk.reshape(KT, 128, 128).transpose(1, 0, 2).reshape(128, KT * 128))
    return out


def _prep_shared(inp):
    f = np.float32
    w_in = np.asarray(inp["w_in"], f)
    w_out = np.asarray(inp["w_out"], f)
    wi_units, wo_units, wg_l = [], [], []
    for l in range(NL):
        cols = []
        for s in range(8):
            cols += [s * 128, 1024 + s * 128]
            cols += [2048 + s * 128, 3072 + s * 128, 4096 + s * 128, 5120 + s * 128]
        wi_units += _tile_units(w_in[l], cols)
        rows = []
        for s in range(8):
            rows += list(range(s * 128, (s + 1) * 128))
            rows += list(range(1024 + s * 128, 1024 + (s + 1) * 128))
        wp = w_out[l][rows, :]
        wo_units += _tile_units(wp, [d * 128 for d in range(16)])
        g = np.zeros((128, 8, 2, 128), f)
        for s in range(8):
            for gi, nm in enumerate(["lru_wr", "lru_wi"]):
                wgt = np.asarray(inp[nm], f)[l]
                for hh in range(2):
                    g[hh * 64:(hh + 1) * 64, s, gi, hh * 64:(hh + 1) * 64] = wgt[2 * s + hh]
        wg_l.append(g.reshape(128, 8 * 2 * 128))
    prm = np.zeros((128, NPRM), f)

    def pp(v, n):
        return np.asarray(v, f).reshape(n, 128).T

    for l in range(NL):
        prm[:, _pc(l, P_G):_pc(l, P_G) + 16] = pp(inp["norm_g"][l], 16)
        caw = np.asarray(inp["conv_a_w"], f)[l]
        for s in range(8):
            for k in range(4):
                prm[:, _pc(l, P_CAW) + s * 4 + k] = caw[k, s * 128:(s + 1) * 128]
        prm[:, _pc(l, P_CAB):_pc(l, P_CAB) + 8] = pp(inp["conv_a_b"][l], 8)
        prm[:, _pc(l, P_BR):_pc(l, P_BR) + 8] = pp(inp["lru_br"][l], 8)
        prm[:, _pc(l, P_BI):_pc(l, P_BI) + 8] = pp(inp["lru_bi"][l], 8)
        prm[:, _pc(l, P_LAM):_pc(l, P_LAM) + 8] = pp(inp["lru_lambda"][l], 8)
        cbw = np.asarray(inp["conv_b_w"], f)[l]
        for s in range(8):
            for k in range(3):
                prm[:, _pc(l, P_CBW) + s * 3 + k] = cbw[k, s * 128:(s + 1) * 128]
    prm[:, P_FG:P_FG + 16] = pp(inp["final_g"], 16)
    return (np.ascontiguousarray(np.concatenate(wi_units, 0)),
            np.ascontiguousarray(np.concatenate(wo_units, 0)),
            np.ascontiguousarray(np.concatenate(wg_l, 0)), prm)


def _make_in_maps(inp):
    f = np.float32
    x = np.asarray(inp["x"], f)
    meta = np.asarray(inp["meta"], f)
    wi, wo, wgm, prm = _prep_shared(inp)
    in_maps = []
    for c in range(8):
        b, half = c // 2, c % 2
        loc = np.zeros((T, D), f)
        tm = np.ones((128, 8), f)
        p = prm.copy()
        if half == 0:
            loc[NPRE:NPRE + NMETA] = meta
            loc[NPRE + NMETA:] = x[b, :SPLIT]
            tm[:, :NPRE] = 0.0
            p[:, P_FLAG] = 0.0
        else:
            loc[:] = x[b, SPLIT - NPRE:]
            p[:, P_FLAG] = 1.0
        in_maps.append({"xT": np.ascontiguousarray(loc.T), "w_in": wi, "w_out": wo, "wg": wgm,
                        "prm": p, "tmask": tm})
    return in_maps


def kernel(**inputs):
    in_maps = _make_in_maps(inputs)
    nc = build_program()
    res = run_bass_kernel_spmd(nc, in_maps, core_ids=list(range(8)))
    out = np.empty((4, SEQ, D), np.float32)
    for c in range(8):
        b, half = c // 2, c % 2
        og = res.results[c]["outT"]
        o = np.empty((D, T), np.float32)
        for off, n in GROUPS:
            blk = og[:, KT * off:KT * (off + n)].reshape(128, KT, n)
            o.reshape(KT, 128, T)[:, :, off:off + n] = blk.transpose(1, 0, 2)
        if half == 0:
            out[b, :SPLIT] = o[:, NPRE + NMETA:].T
        else:
            out[b, SPLIT:] = o[:, NPRE:].T
    return out
```

```python
import numpy as np
from contextlib import ExitStack

import concourse.bass as bass
import concourse.mybir as mybir
from concourse.bass_utils import run_bass_kernel_spmd

F32 = mybir.dt.float32
BF16 = mybir.dt.bfloat16
AF = mybir.ActivationFunctionType
ALU = mybir.AluOpType

D = 2048
KT = 16
NL = 2
SEQ = 4096
NMETA = 16
NPRE = 6
OWN = 2056
T = NPRE + OWN
SPLIT = OWN - NMETA
GROUPS = [(0, 413), (413, 413), (826, 412), (1238, 412), (1650, 412)]
NG = len(GROUPS)
GW = 416
RING = 6
STREAM0 = True
TAILD = 4
P2_PASSES = [[0, 1], [2, 3], [4]]
EPS = 1e-6

PL = 104
def _pc(l, off):
    return l * PL + off
P_G, P_CAW, P_CAB, P_BR, P_BI, P_LAM, P_CBW = 0, 16, 48, 56, 64, 72, 80
P_FG = NL * PL
P_FLAG = P_FG + 16
NPRM = P_FLAG + 1


SELF_SYNC = ("act", "dve")


class Tok:
    __slots__ = ("sem", "val", "key")

    def __init__(self, sem, val, key):
        self.sem, self.val, self.key = sem, val, key


class Buf:
    def __init__(self, name):
        self.name = name
        self.w = {}
        self.r = {}
        self.alias = []


class Eng:
    def __init__(self, name, sem):
        self.name, self.sem, self.count = name, sem, 0
        self.waited = {}
        self.prog = []


class DSem:
    def __init__(self, sem, key):
        self.sem, self.key, self.count = sem, key, 0


class Sched:
    def __init__(self):
        self.engs = {}

    def _collect(self, eng, reads, writes):
        need = {}

        def add(d):
            for key, tok in d.items():
                if key == eng.name and eng.name not in SELF_SYNC:
                    continue
                if key not in need or need[key].val < tok.val:
                    need[key] = tok

        for b in reads:
            for bb in [b] + b.alias:
                add(bb.w)
        for b in writes:
            for bb in [b] + b.alias:
                add(bb.w)
                add(bb.r)
        out = []
        for key, tok in need.items():
            if eng.waited.get(key, 0) >= tok.val:
                continue
            eng.waited[key] = tok.val
            out.append(tok)
        return out

    def _update(self, tok, reads, writes):
        for b in reads:
            old = b.r.get(tok.key)
            if old is None or old.val < tok.val:
                b.r[tok.key] = tok
        for b in writes:
            b.w = {tok.key: tok}
            b.r = {}

    def op(self, eng, fn, reads=(), writes=()):
        waits = self._collect(eng, reads, writes)
        eng.count += 1
        tok = Tok(eng.sem, eng.count, eng.name)

        def run(e, waits=waits, fn=fn, sem=eng.sem):
            for w in waits:
                e.wait_ge(w.sem, w.val)
            ins = fn(e)
            ins.then_inc(sem, 1)

        eng.prog.append(run)
        self._update(tok, reads, writes)
        return tok

    def dma(self, eng, dsem, fn, reads=(), writes=()):
        waits = self._collect(eng, reads, writes)
        dsem.count += 16
        tok = Tok(dsem.sem, dsem.count, dsem.key)

        def run(e, waits=waits, fn=fn, sem=dsem.sem):
            for w in waits:
                e.wait_ge(w.sem, w.val)
            fn(e).then_inc(sem, 16)

        eng.prog.append(run)
        self._update(tok, reads, writes)
        return tok

    def coll(self, eng, dsem, fn, reads=(), writes=()):
        waits = self._collect(eng, reads, writes)
        dsem.count += 1
        tok = Tok(dsem.sem, dsem.count, dsem.key)

        def run(e, waits=waits, fn=fn, sem=dsem.sem):
            for w in waits:
                e.wait_ge(w.sem, w.val)
            fn(e).then_inc(sem)

        eng.prog.append(run)
        self._update(tok, reads, writes)
        return tok

    def final_wait(self, eng, bufs):
        waits = self._collect(eng, [], bufs)

        def run(e, waits=waits):
            for w in waits:
                e.wait_ge(w.sem, w.val)

        eng.prog.append(run)


def build_program(dbg=False):
    nc = bass.Bass("TRN2", target_bir_lowering=False)
    xT = nc.dram_tensor("xT", [D, T], F32, kind="ExternalInput").ap()
    w_in = nc.dram_tensor("w_in", [NL * 48 * 128, KT * 128], F32, kind="ExternalInput").ap()
    w_out = nc.dram_tensor("w_out", [NL * 16 * 128, KT * 128], F32, kind="ExternalInput").ap()
    wg = nc.dram_tensor("wg", [NL * 128, 8 * 2 * 128], F32, kind="ExternalInput").ap()
    prm = nc.dram_tensor("prm", [128, NPRM], F32, kind="ExternalInput").ap()
    tmask = nc.dram_tensor("tmask", [128, 8], F32, kind="ExternalInput").ap()
    outT = nc.dram_tensor("outT", [128, KT * T], F32, kind="ExternalOutput").ap()
    spill = []
    for l in range(NL):
        if dbg:
            spill.append(nc.dram_tensor(f"spill{l}", [D, T], F32, kind="ExternalOutput").ap())
        else:
            spill.append(nc.dram_tensor(f"spill{l}", [D, T], F32).ap())
    if dbg:
        dbg_drv = nc.dram_tensor("dbg_drv", [128, NL * 32], F32, kind="ExternalOutput").ap()
        dbg_ab = nc.dram_tensor("dbg_ab", [128, 3 * T], F32, kind="ExternalOutput").ap()
        dbg_st = nc.dram_tensor("dbg_st", [128, 6], F32, kind="ExternalOutput").ap()
    bnc_in = [[nc.dram_tensor(f"bi_{l}_{s}", [128, 1], F32) for s in range(8)] for l in range(NL)]
    bnc_out = [[nc.dram_tensor(f"bo_{l}_{s}", [256, 1], F32) for s in range(8)] for l in range(NL)]

    S = Sched()
    with ExitStack() as es:
        def sb(name, shape, dt):
            return es.enter_context(nc.sbuf_tensor(name, shape, dt))

        def sem(name):
            return es.enter_context(nc.semaphore(name))

        bufA = sb("bufA", [128, KT * T], BF16)
        bufB = sb("bufB", [128, KT * T], BF16)
        LRUW = max(3 * T, KT * GW)
        lru_big = sb("lru_big", [128, LRUW], F32)
        a_full = lru_big[:, 0:T]
        b_full = lru_big[:, T:2 * T]
        t1_full = lru_big[:, 2 * T:3 * T]
        ring = [sb(f"ring{i}", [128, KT * 128], BF16) for i in range(RING)]
        NTMP = 9
        tmp = [sb(f"tmp{i}", [128, GW], F32) for i in range(NTMP)]
        halo = [sb(f"halo{i}", [128, GW + 4], F32) for i in range(4)]
        wgs = sb("wgs", [128, 8 * 2 * 128], BF16)
        prm_sb = sb("prm_sb", [128, NPRM], F32)
        drv = sb("drv", [128, NL * 32], F32)
        sc = sb("scr", [128, 64], F32)
        tm_sb = sb("tm_sb", [128, 8], F32)
        ones = sb("ones", [128, 128], BF16)
        epsb = sb("epsb", [128, 1], F32)
        stx = sb("stx", [128, 2], F32)
        gath = sb("gath", [128, 2], F32)
        init = sb("init", [128, 2], F32)
        banks = [es.enter_context(nc.psum_tensor(f"bank{i}", [128, 512], F32)) for i in range(8)]

        E = {}
        for nm in ["pe", "act", "dve", "pool", "sp"]:
            E[nm] = Eng(nm, sem("s_" + nm))
        pe, act, dve, pool, sp = E["pe"], E["act"], E["dve"], E["pool"], E["sp"]

        def dsem(name):
            return DSem(sem("d_" + name), "d_" + name)

        ring_ds = [dsem(f"ring{i}") for i in range(RING)]
        tmp_ds = [dsem(f"tmp{i}") for i in range(NTMP)]
        tmp_st_ds = [dsem(f"tst{i}") for i in range(NTMP)]
        wgs_ds = dsem("wgs")
        stage_ds = [dsem("stage0"), dsem("stage1"), dsem("stage2"), dsem("stage3"), dsem("stage4")]
        misc_ds = dsem("misc")
        st_ds = dsem("stx")
        ga_ds = dsem("gath")
        cc_ds = dsem("cc")

        B_ring = [Buf(f"ring{i}") for i in range(RING)]
        B_tmp = [Buf(f"tmp{i}") for i in range(NTMP)]
        B_halo = [Buf(f"halo{i}") for i in range(4)]
        B_bank = [Buf(f"bank{i}") for i in range(8)]
        B_hn = [Buf(f"hn{g}") for g in range(NG)]
        B_y = [[Buf(f"y{k}_{g}") for g in range(NG)] for k in range(KT)]
        B_stage = [Buf("stage0"), Buf("stage1"), Buf("stage2"), Buf("stage3"), Buf("stage4")]
        for si, ks in enumerate([range(2, 9), range(9, 16)]):
            for k in ks:
                for g in range(NG):
                    B_stage[si].alias.append(B_y[k][g])
                    B_y[k][g].alias.append(B_stage[si])
        B_a = [Buf(f"a{g}") for g in range(NG)]
        B_b = [Buf(f"b{g}") for g in range(NG)]
        B_t1 = [Buf(f"t1{g}") for g in range(NG)]
        B_wgs = Buf("wgs")
        B_prm = Buf("prm")
        B_drv = Buf("drv")
        B_const = Buf("const")
        B_stx = [Buf("stx0"), Buf("stx1")]
        B_gath = [Buf("gath0"), Buf("gath1")]
        B_init = [Buf("init0"), Buf("init1")]
        B_spill = [[[Buf(f"sp{l}_{d}_{g}") for g in range(NG)] for d in range(KT)] for l in range(NL)]
        B_out = [Buf(f"outT{g}") for g in range(NG)]
        B_bin = [[Buf(f"bin{l}_{s}") for s in range(8)] for l in range(NL)]
        B_bout = [[Buf(f"bout{l}_{s}") for s in range(8)] for l in range(NL)]

        for g in range(NG):
            for bb in (B_a[g], B_b[g], B_t1[g]):
                B_stage[2].alias.append(bb)
                bb.alias.append(B_stage[2])
            for si in (3, 4):
                B_stage[si].alias.append(B_hn[g])
                B_hn[g].alias.append(B_stage[si])
        SW = KT * GW
        stage_ap = [bufB[:, 2 * T: 9 * T].bitcast(F32), bufB[:, 9 * T: 16 * T].bitcast(F32),
                    lru_big[:, 0:SW],
                    bufA[:, 0: 2 * SW].bitcast(F32), bufA[:, 2 * SW: 4 * SW].bitcast(F32)]

        def stage_k(si, k, n):
            return stage_ap[si][:, k * GW: k * GW + n]

        rot = {"tmp": 0, "bank": 0, "ring": 0}

        def new_tmp():
            i = rot["tmp"]
            rot["tmp"] = (i + 1) % NTMP
            return i

        bank_free = list(range(8))

        def new_bank():
            assert bank_free, "no free PSUM bank at this point of the schedule"
            return bank_free.pop(0)

        def rel_bank(*bs):
            for b in bs:
                assert b not in bank_free
                bank_free.append(b)

        def take_bank(b):
            assert b in bank_free, f"PSUM bank {b} still live"
            bank_free.remove(b)
            return b

        def P(col, n=1):
            return prm_sb[:, col: col + n]

        wunits = []
        for l in range(NL):
            for u in range(48):
                r0 = (l * 48 + u) * 128
                wunits.append(w_in[r0: r0 + 128, :])
            for d in range(16):
                r0 = (l * 16 + d) * 128
                wunits.append(w_out[r0: r0 + 128, :])
        wstate = {"issued": 0}
        unit_slot = {}

        def prefetch(upto, extra_reads=()):
            upto = min(upto, len(wunits))
            while wstate["issued"] < upto:
                n = wstate["issued"]
                slot = n % RING
                unit_slot[n] = slot
                src = wunits[n]
                S.dma(pool, ring_ds[slot],
                      lambda e, slot=slot, src=src: e.dma_start(out=ring[slot][:, :], in_=src),
                      reads=list(extra_reads), writes=[B_ring[slot]])
                wstate["issued"] += 1

        wpos = {"n": 0}

        def take_units(n):
            base = wpos["n"]
            wpos["n"] += n
            prefetch(base + n)
            return [unit_slot[base + i] for i in range(n)]

        S.dma(sp, misc_ds, lambda e: e.dma_start(out=prm_sb[:, :], in_=prm), writes=[B_prm])
        S.dma(sp, misc_ds, lambda e: e.dma_start(out=tm_sb[:, :], in_=tmask), writes=[B_prm])
        S.op(dve, lambda e: e.memset(epsb[:, :], EPS), writes=[B_const])
        S.op(dve, lambda e: e.memset(ones[:, :], 1.0), writes=[B_const])
        for i in range(4):
            S.op(dve, lambda e, i=i: e.memset(halo[i][:, :], 0.0), writes=[B_halo[i]])

        def derive():
            for l in range(NL):
                derive_layer(l)

        def derive_layer(l):
            if True:
                lam = P(_pc(l, P_LAM), 8)
                o = l * 32
                ee, uu, dd, lnu, rr, zz = (sc[:, 0:8], sc[:, 8:16], sc[:, 16:24], sc[:, 24:32],
                                           sc[:, 32:40], sc[:, 40:48])
                S.op(act, lambda e: e.activation(out=ee, in_=lam, func=AF.Exp, scale=-1.0),
                     reads=[B_prm], writes=[B_drv])
                S.op(dve, lambda e: e.tensor_scalar(out=uu, in0=ee, scalar1=1.0, scalar2=None, op0=ALU.add),
                     reads=[B_drv], writes=[B_drv])
                S.op(dve, lambda e: e.tensor_scalar(out=dd, in0=uu, scalar1=-1.0, scalar2=None, op0=ALU.add),
                     writes=[B_drv])
                S.op(dve, lambda e: e.tensor_scalar(out=zz, in0=dd, scalar1=0.0, scalar2=None, op0=ALU.is_equal),
                     writes=[B_drv])
                S.op(dve, lambda e: e.tensor_scalar(out=dd, in0=dd, scalar1=1e-30, scalar2=None, op0=ALU.max),
                     writes=[B_drv])
                S.op(dve, lambda e: e.reciprocal(out=rr, in_=dd), writes=[B_drv])
                S.op(act, lambda e: e.activation(out=lnu, in_=uu, func=AF.Ln), reads=[B_drv], writes=[B_drv])
                S.op(dve, lambda e: e.tensor_tensor(out=rr, in0=rr, in1=lnu, op=ALU.mult),
                     reads=[B_drv], writes=[B_drv])
                S.op(dve, lambda e: e.tensor_tensor(out=rr, in0=rr, in1=zz, op=ALU.add), writes=[B_drv])
                S.op(dve, lambda e: e.tensor_tensor(out=rr, in0=rr, in1=ee, op=ALU.mult), writes=[B_drv])
                S.op(dve, lambda e, o=o: e.tensor_scalar(out=drv[:, o + 16: o + 24], in0=rr, scalar1=-8.0,
                                                         scalar2=None, op0=ALU.mult), writes=[B_drv])
                S.op(dve, lambda e, o=o: e.tensor_scalar(out=drv[:, o + 24: o + 32], in0=rr, scalar1=-4.0,
                                                         scalar2=None, op0=ALU.mult), writes=[B_drv])
                S.op(dve, lambda e, o=o, l=l: e.tensor_scalar(out=drv[:, o: o + 8], in0=P(_pc(l, P_BR), 8),
                                                              scalar1=0.5, scalar2=None, op0=ALU.mult),
                     writes=[B_drv])
                S.op(dve, lambda e, o=o, l=l: e.tensor_scalar(out=drv[:, o + 8: o + 16], in0=P(_pc(l, P_BI), 8),
                                                              scalar1=0.5, scalar2=None, op0=ALU.mult),
                     writes=[B_drv])

        derive()

        def DV(l, which, s):
            o = l * 32 + which * 8 + s
            return drv[:, o: o + 1]

        def phase0_load(src_ap, src_bufs, g, si):
            off, n = GROUPS[g]
            S.dma(sp, stage_ds[si],
                  lambda e: e.dma_start(
                      out=stage_ap[si][:, 0: KT * GW].rearrange("p (k w) -> p k w", k=KT)[:, :, 0:n],
                      in_=src_ap.rearrange("(k p) t -> p k t", p=128)[:, :, off: off + n]),
                  reads=src_bufs, writes=[B_stage[si]])

        def phase0_group(src_ap, src_bufs, g, si, gcol0, final):
            phase0_load(src_ap, src_bufs, g, si)
            phase0_compute(g, si, gcol0, final)

        def phase0_compute(g, si, gcol0, final):
            off, n = GROUPS[g]
            bk = new_bank()
            for k in range(KT):
                ti = new_tmp()
                S.op(act, lambda e, k=k, ti=ti: e.activation(out=tmp[ti][:, 0:n].bitcast(BF16)[:, 0:n],
                                                             in_=stage_k(si, k, n), func=AF.Square),
                     reads=[B_stage[si]], writes=[B_tmp[ti]])
                S.op(pe, lambda e, k=k, ti=ti: e.matmul(banks[bk][:, 0:n], lhsT=ones[:, :],
                                                        rhs=tmp[ti][:, 0:n].bitcast(BF16)[:, 0:n],
                                                        start=(k == 0), stop=(k == KT - 1)),
                     reads=[B_tmp[ti], B_const], writes=[B_bank[bk]])
            ti = new_tmp()
            S.op(act, lambda e: e.activation(out=tmp[ti][:, 0:n], in_=banks[bk][:, 0:n], func=AF.Sqrt,
                                             bias=epsb[:, 0:1], scale=1.0 / D),
                 reads=[B_bank[bk], B_const], writes=[B_tmp[ti]])
            S.op(dve, lambda e: e.reciprocal(out=banks[bk][:, 0:n], in_=tmp[ti][:, 0:n]),
                 reads=[B_tmp[ti]], writes=[B_bank[bk]])
            for k in range(KT):
                if not final:
                    S.op(dve, lambda e, k=k: e.scalar_tensor_tensor(
                        out=bufA[:, k * T + off: k * T + off + n], in0=stage_k(si, k, n),
                        scalar=P(gcol0 + k), in1=banks[bk][:, 0:n], op0=ALU.mult, op1=ALU.mult),
                        reads=[B_stage[si], B_bank[bk], B_prm], writes=[B_hn[g]])
                else:
                    S.op(dve, lambda e, k=k: e.scalar_tensor_tensor(
                        out=stage_k(si, k, n), in0=stage_k(si, k, n),
                        scalar=P(gcol0 + k), in1=banks[bk][:, 0:n], op0=ALU.mult, op1=ALU.mult),
                        reads=[B_bank[bk], B_prm], writes=[B_stage[si]])
            if final:
                S.dma(pool, stage_ds[si],
                      lambda e: e.dma_start(
                          out=outT.rearrange("(k p) t -> p k t", p=128)[:, :, off: off + n],
                          in_=stage_ap[si][:, 0: KT * GW].rearrange("p (k w) -> p k w", k=KT)[:, :, 0:n]),
                      reads=[B_stage[si]], writes=[B_out[g]])
            rel_bank(bk)

        def mm_group(bk, unit, g):
            off, n = GROUPS[g]

            def fn(e):
                ins = None
                for k in range(KT):
                    ins = e.matmul(banks[bk][:, 0:n], lhsT=ring[unit][:, k * 128:(k + 1) * 128],
                                   rhs=bufA[:, k * T + off: k * T + off + n],
                                   start=(k == 0), stop=(k == KT - 1))
                return ins
            S.op(pe, fn, reads=[B_ring[unit], B_hn[g]], writes=[B_bank[bk]])

        def lru_main(l, s, g, units):
            bks = [new_bank(), new_bank()]
            mm_group(bks[0], units[0], g)
            mm_group(bks[1], units[1], g)
            return bks

        def xcb_view(off, n):
            return lru_big[:, off:off + n].bitcast(BF16)[:, 0:n]

        def lru_stage1(l, s, g, bks):
            off, n = GROUPS[g]
            hb = halo[g % 2]
            hbn = halo[(g + 1) % 2]
            Bh, Bhn = B_halo[g % 2], B_halo[(g + 1) % 2]
            xa_ps, ga_ps = banks[bks[0]], banks[bks[1]]
            Bxa, Bga = B_bank[bks[0]], B_bank[bks[1]]
            cw = _pc(l, P_CAW) + s * 4
            if g == 0:
                S.op(dve, lambda e: e.memset(hb[:, 0:3], 0.0), writes=[Bh])
            S.op(act, lambda e: e.activation(out=hb[:, 3:3 + n], in_=xa_ps[:, 0:n], func=AF.Identity),
                 reads=[Bxa], writes=[Bh])
            S.op(act, lambda e: e.activation(out=xa_ps[:, 0:n], in_=xa_ps[:, 0:n], func=AF.Identity,
                                             scale=P(cw + 3), bias=P(_pc(l, P_CAB) + s)),
                 reads=[B_prm], writes=[Bxa])
            thg = new_tmp()
            S.op(act, lambda e: e.activation(out=tmp[thg][:, 0:n], in_=ga_ps[:, 0:n], func=AF.Tanh, scale=0.5),
                 reads=[Bga], writes=[B_tmp[thg]])
            if g + 1 < NG:
                S.op(act, lambda e: e.activation(out=hbn[:, 0:3], in_=hb[:, n:n + 3], func=AF.Identity),
                     reads=[Bh], writes=[Bhn])
            for tap in (2, 1):
                S.op(dve, lambda e, tap=tap: e.scalar_tensor_tensor(
                    out=xa_ps[:, 0:n], in0=hb[:, tap:tap + n], scalar=P(cw + tap), in1=xa_ps[:, 0:n],
                    op0=ALU.mult, op1=ALU.add),
                    reads=[Bh, B_prm], writes=[Bxa])
            S.op(dve, lambda e: e.scalar_tensor_tensor(
                out=b_full[:, off:off + n], in0=hb[:, 0:n], scalar=P(cw + 0), in1=xa_ps[:, 0:n],
                op0=ALU.mult, op1=ALU.add),
                reads=[Bh, Bxa, B_prm], writes=[B_b[g]])
            S.op(dve, lambda e: e.scalar_tensor_tensor(
                out=t1_full[:, off:off + n], in0=tmp[thg][:, 0:n], scalar=1.0, in1=ga_ps[:, 0:n],
                op0=ALU.add, op1=ALU.mult),
                reads=[B_tmp[thg], Bga], writes=[B_t1[g]])
            S.op(act, lambda e: e.activation(out=xcb_view(off, n), in_=b_full[:, off:off + n],
                                             func=AF.Identity),
                 reads=[B_b[g]], writes=[B_a[g]])
            rel_bank(*bks)

        def lru_gates(l, s, g):
            off, n = GROUPS[g]
            gb2 = [new_bank(), new_bank()]
            for gi in range(2):
                S.op(pe, lambda e, gi=gi: e.matmul(
                    banks[gb2[gi]][:, 0:n], lhsT=wgs[:, (s * 2 + gi) * 128:(s * 2 + gi + 1) * 128],
                    rhs=xcb_view(off, n), start=True, stop=True),
                    reads=[B_wgs, B_a[g]], writes=[B_bank[gb2[gi]]])
            return gb2

        def lru_stage2(l, s, g, gb2):
            off, n = GROUPS[g]
            r_ps, i_ps = banks[gb2[0]], banks[gb2[1]]
            Br, Bi = B_bank[gb2[0]], B_bank[gb2[1]]
            t_r, t_i = new_tmp(), new_tmp()
            S.op(act, lambda e: e.activation(out=tmp[t_r][:, 0:n], in_=r_ps[:, 0:n], func=AF.Tanh,
                                             scale=0.5, bias=DV(l, 0, s)),
                 reads=[Br, B_drv], writes=[B_tmp[t_r]])
            S.op(act, lambda e: e.activation(out=tmp[t_i][:, 0:n], in_=i_ps[:, 0:n], func=AF.Tanh,
                                             scale=0.5, bias=DV(l, 1, s)),
                 reads=[Bi, B_drv], writes=[B_tmp[t_i]])
            S.op(act, lambda e: e.activation(out=a_full[:, off:off + n], in_=tmp[t_r][:, 0:n], func=AF.Exp,
                                             scale=DV(l, 3, s), bias=DV(l, 3, s)),
                 reads=[B_tmp[t_r], B_drv], writes=[B_a[g]])
            S.op(act, lambda e: e.activation(out=tmp[t_r][:, 0:n], in_=tmp[t_r][:, 0:n], func=AF.Exp,
                                             scale=DV(l, 2, s), bias=DV(l, 2, s)),
                 reads=[B_drv], writes=[B_tmp[t_r]])
            S.op(act, lambda e: e.activation(out=tmp[t_r][:, 0:n], in_=tmp[t_r][:, 0:n], func=AF.Sqrt,
                                             scale=-0.25, bias=0.25),
                 reads=[], writes=[B_tmp[t_r]])
            S.op(dve, lambda e: e.scalar_tensor_tensor(
                out=tmp[t_i][:, 0:n], in0=tmp[t_i][:, 0:n], scalar=1.0, in1=b_full[:, off:off + n],
                op0=ALU.add, op1=ALU.mult),
                reads=[B_b[g]], writes=[B_tmp[t_i]])
            S.op(dve, lambda e: e.tensor_tensor(out=b_full[:, off:off + n], in0=tmp[t_i][:, 0:n],
                                                in1=tmp[t_r][:, 0:n], op=ALU.mult),
                 reads=[B_tmp[t_i], B_tmp[t_r]], writes=[B_b[g]])
            if g == 0:
                S.op(dve, lambda e: e.tensor_tensor(out=b_full[:, 0:8], in0=b_full[:, 0:8], in1=tm_sb[:, 0:8],
                                                    op=ALU.mult),
                     reads=[B_prm], writes=[B_b[g]])
            rel_bank(*gb2)

        def scan_pass(l, s, ybase, which):
            s0 = 3 * (l + 1)
            par = s % 2
            prev = None
            if which == 2:
                S.op(dve, lambda e: e.tensor_tensor(out=init[:, par:par + 1], in0=gath[:, par:par + 1],
                                                    in1=P(P_FLAG), op=ALU.mult),
                     reads=[B_gath[par], B_prm], writes=[B_init[par]])
            for g in range(NG):
                off, n = GROUPS[g]
                lo = s0 if g == 0 else 0
                scr = new_tmp()
                if g == 0:
                    if which == 2:
                        S.op(dve, lambda e, scr=scr, lo=lo: e.memset(tmp[scr][:, 0:lo], 0.0), writes=[B_tmp[scr]])
                        ini = init[:, par:par + 1]
                        inib = [B_init[par]]
                    else:
                        ini = 0.0
                        inib = []
                else:
                    pn = GROUPS[g - 1][1]
                    ini = tmp[prev][:, pn - 1:pn]
                    inib = [B_tmp[prev]]
                S.op(dve, lambda e, scr=scr, ini=ini, lo=lo, off=off, n=n: e.tensor_tensor_scan(
                    out=tmp[scr][:, lo:n], data0=a_full[:, off + lo:off + n], data1=b_full[:, off + lo:off + n],
                    initial=ini, op0=ALU.mult, op1=ALU.add),
                    reads=[B_a[g], B_b[g]] + inib, writes=[B_tmp[scr]])
                if which == 2:
                    S.op(dve, lambda e, scr=scr, off=off, n=n: e.scalar_tensor_tensor(
                        out=bufB[:, ybase * T + off: ybase * T + off + n], in0=tmp[scr][:, 0:n], scalar=0.5,
                        in1=t1_full[:, off:off + n], op0=ALU.mult, op1=ALU.mult),
                        reads=[B_tmp[scr], B_t1[g]], writes=[B_y[ybase][g]])
                prev = scr
            if which == 1:
                n = GROUPS[NG - 1][1]
                col = n - 4 if l == 0 else n - 1
                S.op(dve, lambda e: e.tensor_copy(out=stx[:, par:par + 1], in_=tmp[prev][:, col:col + 1]),
                     reads=[B_tmp[prev]], writes=[B_stx[par]])
                S.dma(sp, st_ds, lambda e: e.dma_start(out=bnc_in[l][s].ap(), in_=stx[:, par:par + 1]),
                      reads=[B_stx[par]], writes=[B_bin[l][s]])

        def exchange(l, s):
            par = s % 2
            S.coll(pool, cc_ds, lambda e: e.collective_compute(
                "AllGather", ALU.bypass, replica_groups=[[0, 1], [2, 3], [4, 5], [6, 7]],
                ins=[bnc_in[l][s].ap().opt()], outs=[bnc_out[l][s].ap().opt()]),
                reads=[B_bin[l][s]], writes=[B_bout[l][s]])
            S.dma(sp, ga_ds, lambda e: e.dma_start(out=gath[:, par:par + 1], in_=bnc_out[l][s].ap()[0:128, :]),
                  reads=[B_bout[l][s]], writes=[B_gath[par]])

        def conv_main(l, s, g, units):
            bks = [new_bank() for _ in range(4)]
            for j in range(4):
                mm_group(bks[j], units[j], g)
            return bks

        def conv_stage(l, s, g, bks, ybase):
            off, n = GROUPS[g]
            hb = halo[2 + g % 2]
            hbn = halo[2 + (g + 1) % 2]
            Bh, Bhn = B_halo[2 + g % 2], B_halo[2 + (g + 1) % 2]
            gb_ps_, gc_ps, xb_ps, gbk = banks[bks[0]], banks[bks[1]], banks[bks[2]], banks[bks[3]]
            Bgb, Bgc, Bxb, Bgk = B_bank[bks[0]], B_bank[bks[1]], B_bank[bks[2]], B_bank[bks[3]]
            cw = _pc(l, P_CBW) + s * 3
            if g == 0:
                S.op(dve, lambda e: e.memset(hb[:, 0:2], 0.0), writes=[Bh])
            xb = new_tmp()
            S.op(act, lambda e: e.activation(out=tmp[xb][:, 0:n], in_=xb_ps[:, 0:n], func=AF.Identity),
                 reads=[Bxb], writes=[B_tmp[xb]])
            thg = new_tmp()
            S.op(act, lambda e: e.activation(out=tmp[thg][:, 0:n], in_=gbk[:, 0:n], func=AF.Tanh, scale=0.5),
                 reads=[Bgk], writes=[B_tmp[thg]])
            S.op(dve, lambda e: e.tensor_tensor(out=hb[:, 2:2 + n], in0=tmp[xb][:, 0:n], in1=gc_ps[:, 0:n],
                                                op=ALU.mult),
                 reads=[B_tmp[xb], Bgc], writes=[Bh])
            S.op(act, lambda e: e.activation(out=gc_ps[:, 0:n], in_=hb[:, 2:2 + n], func=AF.Identity,
                                             scale=P(cw + 2)),
                 reads=[Bh, B_prm], writes=[Bgc])
            if g + 1 < NG:
                S.op(act, lambda e: e.activation(out=hbn[:, 0:2], in_=hb[:, n:n + 2], func=AF.Identity),
                     reads=[Bh], writes=[Bhn])
            S.op(dve, lambda e: e.scalar_tensor_tensor(
                out=gc_ps[:, 0:n], in0=hb[:, 1:1 + n], scalar=P(cw + 1), in1=gc_ps[:, 0:n],
                op0=ALU.mult, op1=ALU.add),
                reads=[Bh, B_prm], writes=[Bgc])
            q2 = new_tmp()
            S.op(dve, lambda e: e.scalar_tensor_tensor(
                out=tmp[q2][:, 0:n], in0=hb[:, 0:n], scalar=P(cw + 0), in1=gc_ps[:, 0:n],
                op0=ALU.mult, op1=ALU.add),
                reads=[Bh, Bgc, B_prm], writes=[B_tmp[q2]])
            S.op(dve, lambda e: e.tensor_tensor(out=tmp[q2][:, 0:n], in0=tmp[q2][:, 0:n], in1=gb_ps_[:, 0:n],
                                                op=ALU.mult),
                 reads=[Bgb], writes=[B_tmp[q2]])
            S.op(dve, lambda e: e.scalar_tensor_tensor(
                out=xb_ps[:, 0:n], in0=tmp[thg][:, 0:n], scalar=1.0, in1=gbk[:, 0:n],
                op0=ALU.add, op1=ALU.mult),
                reads=[B_tmp[thg], Bgk], writes=[Bxb])
            S.op(dve, lambda e: e.scalar_tensor_tensor(
                out=bufB[:, ybase * T + off: ybase * T + off + n], in0=tmp[q2][:, 0:n], scalar=0.5,
                in1=xb_ps[:, 0:n], op0=ALU.mult, op1=ALU.mult),
                reads=[B_tmp[q2], Bxb], writes=[B_y[ybase][g]])
            rel_bank(*bks)

        def phase2_item(l, d, g, unit, src_ap, src_buf):
            off, n = GROUPS[g]
            hin = new_tmp()
            S.dma(sp, tmp_ds[hin],
                  lambda e: e.dma_start(out=tmp[hin][:, 0:n], in_=src_ap[d * 128:(d + 1) * 128, off:off + n]),
                  reads=src_buf, writes=[B_tmp[hin]])
            bk = new_bank()

            def fn(e):
                ins = None
                for k in range(KT):
                    ins = e.matmul(banks[bk][:, 0:n], lhsT=ring[unit][:, k * 128:(k + 1) * 128],
                                   rhs=bufB[:, k * T + off: k * T + off + n],
                                   start=(k == 0), stop=(k == KT - 1))
                return ins
            S.op(pe, fn, reads=[B_ring[unit]] + [B_y[k][g] for k in range(KT)], writes=[B_bank[bk]])
            S.op(dve, lambda e: e.tensor_tensor(out=tmp[hin][:, 0:n], in0=tmp[hin][:, 0:n],
                                                in1=banks[bk][:, 0:n], op=ALU.add),
                 reads=[B_bank[bk]], writes=[B_tmp[hin]])
            S.dma(pool, tmp_st_ds[hin],
                  lambda e: e.dma_start(out=spill[l][d * 128:(d + 1) * 128, off:off + n], in_=tmp[hin][:, 0:n]),
                  reads=[B_tmp[hin]], writes=[B_spill[l][d][g]])

        bufA_f32 = bufA[:, :].bitcast(F32)
        wgs_f32 = wgs[:, :].bitcast(F32)
        NHOLD = KT - TAILD
        B_hold = [[Buf(f"hold{d}_{g}") for g in range(NG)] for d in range(NHOLD)]
        for d in range(NHOLD):
            for g in range(NG):
                if d < 8:
                    others = B_hn
                elif d < 11:
                    others = B_a + B_b + B_t1
                else:
                    others = [B_halo[g]] if g < 4 else [B_wgs]
                for ob in others:
                    B_hold[d][g].alias.append(ob)
                    ob.alias.append(B_hold[d][g])
        B_out2 = []
        out_ds = dsem("outs")

        def hold_ap(d, g, off, n):
            if d < 8:
                return bufA_f32[:, 8 * off + d * n:8 * off + (d + 1) * n]
            if d < 11:
                return lru_big[:, 3 * off + (d - 8) * n:3 * off + (d - 7) * n]
            if g < 4:
                return halo[g][:, 0:n]
            return wgs_f32[:, 0:n]

        def out_ap(g, d0, d1):
            off, n = GROUPS[g]
            return outT[:, KT * off + d0 * n:KT * off + d1 * n]

        def phase2_v2(l, src_ap, src_bufs):
            last = (l == NL - 1)
            SQ = [take_bank(b) for b in (0, 1, 2, 3, 4)]
            DL = [take_bank(b) for b in (5, 6, 7)]
            st = {"dl": 0}
            gnext = P_FG if last else _pc(l + 1, P_G)
            pend = []
            tail_tmps = [[] for _ in range(NG)]
            free_t = list(range(NTMP))
            free_h = []

            def t_alloc():
                if not last:
                    return new_tmp()
                assert free_t, "no free temp"
                return free_t.pop(0)

            def t_free(i):
                if last:
                    free_t.append(i)

            def h_alloc():
                if not last:
                    return (new_tmp(), 0)
                if not free_h:
                    i = t_alloc()
                    free_h.extend([(i, 0), (i, 1)])
                return free_h.pop(0)

            def h_free(x):
                if last:
                    free_h.append(x)

            def sq_ap(x, n):
                i, h = x
                return tmp[i][:, h * (GW // 2):(h + 1) * (GW // 2)].bitcast(BF16)[:, 0:n]

            def flush_one():
                d, g, sq = pend.pop(0)
                off, n = GROUPS[g]
                S.op(pe, lambda e: e.matmul(banks[SQ[g]][:, 0:n], lhsT=ones[:, :],
                                            rhs=sq_ap(sq, n),
                                            start=(d == 0), stop=(d == KT - 1)),
                     reads=[B_tmp[sq[0]], B_const], writes=[B_bank[SQ[g]]])
                h_free(sq)

            def item(d, g, unit):
                off, n = GROUPS[g]
                hin = t_alloc()
                S.dma(sp, tmp_ds[hin],
                      lambda e: e.dma_start(out=tmp[hin][:, 0:n], in_=src_ap[d * 128:(d + 1) * 128, off:off + n]),
                      reads=(src_bufs[g] if l > 0 else []), writes=[B_tmp[hin]])
                bk = DL[st["dl"] % len(DL)]
                st["dl"] += 1

                def fn(e):
                    ins = None
                    for k in range(KT):
                        ins = e.matmul(banks[bk][:, 0:n], lhsT=ring[unit][:, k * 128:(k + 1) * 128],
                                       rhs=bufB[:, k * T + off: k * T + off + n],
                                       start=(k == 0), stop=(k == KT - 1))
                    return ins
                S.op(pe, fn, reads=[B_ring[unit]] + [B_y[k][g] for k in range(KT)], writes=[B_bank[bk]])
                if len(pend) >= 2:
                    flush_one()
                S.op(dve, lambda e: e.tensor_tensor(out=tmp[hin][:, 0:n], in0=tmp[hin][:, 0:n],
                                                    in1=banks[bk][:, 0:n], op=ALU.add),
                     reads=[B_bank[bk]], writes=[B_tmp[hin]])
                sq = h_alloc()
                S.op(act, lambda e: e.activation(out=sq_ap(sq, n), in_=tmp[hin][:, 0:n],
                                                 func=AF.Square),
                     reads=[B_tmp[hin]], writes=[B_tmp[sq[0]]])
                pend.append((d, g, sq))
                if not last:
                    S.dma(pool, tmp_st_ds[hin],
                          lambda e: e.dma_start(out=spill[l][d * 128:(d + 1) * 128, off:off + n],
                                                in_=tmp[hin][:, 0:n]),
                          reads=[B_tmp[hin]], writes=[B_spill[l][d][g]])
                    S.op(act, lambda e: e.activation(out=bufA[:, d * T + off: d * T + off + n],
                                                     in_=tmp[hin][:, 0:n], func=AF.Identity,
                                                     scale=P(gnext + d)),
                         reads=[B_tmp[hin], B_prm], writes=[B_hn[g]])
                elif d < NHOLD:
                    S.op(act, lambda e: e.activation(out=hold_ap(d, g, off, n), in_=tmp[hin][:, 0:n],
                                                     func=AF.Identity, scale=P(gnext + d)),
                         reads=[B_tmp[hin], B_prm], writes=[B_hold[d][g]])
                    t_free(hin)
                else:
                    S.op(act, lambda e: e.activation(out=tmp[hin][:, 0:n], in_=tmp[hin][:, 0:n],
                                                     func=AF.Identity, scale=P(gnext + d)),
                         reads=[B_prm], writes=[B_tmp[hin]])
                    tail_tmps[g].append((d, hin))

            def finalize_chunks(g):
                off, n = GROUPS[g]
                bk = SQ[g]

                def c_rstd():
                    ti = t_alloc()
                    S.op(act, lambda e: e.activation(out=tmp[ti][:, 0:n], in_=banks[bk][:, 0:n], func=AF.Ln,
                                                     bias=epsb[:, 0:1], scale=1.0 / D),
                         reads=[B_bank[bk], B_const], writes=[B_tmp[ti]])
                    S.op(act, lambda e: e.activation(out=banks[bk][:, 0:n], in_=tmp[ti][:, 0:n], func=AF.Exp,
                                                     scale=-0.5),
                         reads=[B_tmp[ti]], writes=[B_bank[bk]])
                    t_free(ti)

                def c_rescale(k0, k1):
                    for k in range(k0, k1):
                        S.op(dve, lambda e, k=k: e.tensor_tensor(
                            out=bufA[:, k * T + off: k * T + off + n], in0=bufA[:, k * T + off: k * T + off + n],
                            in1=banks[bk][:, 0:n], op=ALU.mult),
                            reads=[B_bank[bk]], writes=[B_hn[g]])

                def c_tail():
                    for d, hin in tail_tmps[g]:
                        S.op(dve, lambda e, hin=hin: e.tensor_tensor(out=tmp[hin][:, 0:n], in0=tmp[hin][:, 0:n],
                                                                     in1=banks[bk][:, 0:n], op=ALU.mult),
                             reads=[B_bank[bk]], writes=[B_tmp[hin]])
                        bo = Buf(f"o2d_{g}_{d}")
                        B_out2.append(bo)
                        S.dma(pool, tmp_st_ds[hin],
                              lambda e, d=d, hin=hin: e.dma_start(out=out_ap(g, d, d + 1),
                                                                  in_=tmp[hin][:, 0:n]),
                              reads=[B_tmp[hin]], writes=[bo])
                        t_free(hin)
                    del tail_tmps[g][:]

                def c_holds(d0, d1):
                    for d in range(d0, d1):
                        S.op(dve, lambda e, d=d: e.tensor_tensor(out=hold_ap(d, g, off, n),
                                                                 in0=hold_ap(d, g, off, n),
                                                                 in1=banks[bk][:, 0:n], op=ALU.mult),
                             reads=[B_bank[bk]], writes=[B_hold[d][g]])

                def c_store_a():
                    bo = Buf(f"o2a_{g}")
                    B_out2.append(bo)
                    S.dma(pool, out_ds, lambda e: e.dma_start(
                        out=out_ap(g, 0, 8), in_=bufA_f32[:, 8 * off:8 * off + 8 * n], max_dma_last_dim=4096),
                        reads=[B_hold[d][g] for d in range(0, 8)], writes=[bo])

                def c_store_bc():
                    bo = Buf(f"o2b_{g}")
                    B_out2.append(bo)
                    S.dma(pool, out_ds, lambda e: e.dma_start(
                        out=out_ap(g, 8, 11), in_=lru_big[:, 3 * off:3 * off + 3 * n], max_dma_last_dim=4096),
                        reads=[B_hold[d][g] for d in range(8, 11)], writes=[bo])
                    bo = Buf(f"o2c_{g}")
                    B_out2.append(bo)
                    S.dma(pool, out_ds, lambda e: e.dma_start(
                        out=out_ap(g, 11, 12), in_=hold_ap(11, g, off, n)),
                        reads=[B_hold[11][g]], writes=[bo])

                if not last:
                    return [lambda: (c_rstd(), c_rescale(0, 4)), lambda: c_rescale(4, 8),
                            lambda: c_rescale(8, 12), lambda: c_rescale(12, 16)]
                return [lambda: (c_rstd(), c_tail()), lambda: c_holds(0, 4),
                        lambda: (c_holds(4, 8), c_store_a()), lambda: (c_holds(8, 12), c_store_bc())]

            for d in range(KT - TAILD):
                unit = take_units(1)[0]
                for g in range(NG):
                    item(d, g, unit)
                prefetch(wpos["n"] + RING)
            units = take_units(TAILD)
            chunks = []
            for g in range(NG):
                for j in range(TAILD):
                    item(KT - TAILD + j, g, units[j])
                    if chunks:
                        chunks.pop(0)()
                        if not chunks:
                            DL.append(SQ[g - 1])
                while pend:
                    flush_one()
                chunks = finalize_chunks(g)
            for c in chunks:
                c()
            prefetch(wpos["n"] + RING)
            rel_bank(*sorted(set(SQ + DL)))

        def phase0_stream_group(g):
            off, n = GROUPS[g]
            bk = new_bank()
            for d in range(KT):
                hin = new_tmp()
                S.dma(sp, tmp_ds[hin],
                      lambda e, d=d, hin=hin: e.dma_start(out=tmp[hin][:, 0:n],
                                                          in_=xT[d * 128:(d + 1) * 128, off:off + n]),
                      reads=[], writes=[B_tmp[hin]])
                sq = new_tmp()
                S.op(act, lambda e, hin=hin, sq=sq: e.activation(out=tmp[sq][:, 0:n].bitcast(BF16)[:, 0:n],
                                                                 in_=tmp[hin][:, 0:n], func=AF.Square),
                     reads=[B_tmp[hin]], writes=[B_tmp[sq]])
                S.op(pe, lambda e, d=d, sq=sq: e.matmul(banks[bk][:, 0:n], lhsT=ones[:, :],
                                                        rhs=tmp[sq][:, 0:n].bitcast(BF16)[:, 0:n],
                                                        start=(d == 0), stop=(d == KT - 1)),
                     reads=[B_tmp[sq], B_const], writes=[B_bank[bk]])
                S.op(act, lambda e, d=d, hin=hin: e.activation(out=bufA[:, d * T + off: d * T + off + n],
                                                               in_=tmp[hin][:, 0:n], func=AF.Identity,
                                                               scale=P(_pc(0, P_G) + d)),
                     reads=[B_tmp[hin], B_prm], writes=[B_hn[g]])
            ti = new_tmp()
            S.op(act, lambda e: e.activation(out=tmp[ti][:, 0:n], in_=banks[bk][:, 0:n], func=AF.Sqrt,
                                             bias=epsb[:, 0:1], scale=1.0 / D),
                 reads=[B_bank[bk], B_const], writes=[B_tmp[ti]])
            S.op(dve, lambda e: e.reciprocal(out=banks[bk][:, 0:n], in_=tmp[ti][:, 0:n]),
                 reads=[B_tmp[ti]], writes=[B_bank[bk]])
            for k in range(KT):
                S.op(dve, lambda e, k=k: e.tensor_tensor(
                    out=bufA[:, k * T + off: k * T + off + n], in0=bufA[:, k * T + off: k * T + off + n],
                    in1=banks[bk][:, 0:n], op=ALU.mult),
                    reads=[B_bank[bk]], writes=[B_hn[g]])
            rel_bank(bk)

        def first_pair(gcol0):
            uL = take_units(2)
            phase0_load(xT, [], 0, 0)
            prefetch(RING, extra_reads=[B_stage[0]])
            uC = take_units(4)
            phase0_load(xT, [B_stage[0]], 1, 1)
            phase0_compute(0, 0, gcol0, False)
            queue = []
            prev_c = None
            for g in range(NG):
                if prev_c is not None:
                    conv_stage(0, 0, prev_c[0], prev_c[1], 1)
                bL = lru_main(0, 0, g, uL)
                if g == NG - 1:
                    prefetch(wpos["n"] + 2)
                lru_stage1(0, 0, g, bL)
                if g + 1 < NG:
                    phase0_compute(g + 1, (g + 1) % 2, gcol0, False)
                bC = conv_main(0, 0, g, uC)
                prev_c = (g, bC)
                lru_stage2(0, 0, g, lru_gates(0, 0, g))
                if g + 2 < NG:
                    phase0_load(xT, [], g + 2, g % 2)
            prefetch(wpos["n"] + RING)
            scan_pass(0, 0, 0, 1)
            exchange(0, 0)
            conv_stage(0, 0, prev_c[0], prev_c[1], 1)
            scan_pass(0, 0, 0, 2)

        dbg_bufs = []
        prefetch(2)
        for l in range(NL):
            S.dma(pool, wgs_ds, lambda e, l=l: e.dma_start(out=wgs[:, :], in_=wg[l * 128:(l + 1) * 128, :]),
                  writes=[B_wgs])
            if l == 0:
                src_ap, src_bufs = xT, [[] for _ in range(NG)]
            else:
                src_ap = spill[l - 1]
                src_bufs = [[B_spill[l - 1][d][g] for d in range(KT)] for g in range(NG)]
            gcol0 = _pc(l, P_G)
            first_slot = 0
            if l == 0:
                first_pair(gcol0)
                first_slot = 2
            pending2 = None
            deferred = []
            for slot in range(first_slot, 16):
                s = slot // 2
                ybase = slot
                if slot % 2 == 0:
                    units = take_units(2)
                    lag = 2
                    queue = []
                    for g in range(NG):
                        bks = lru_main(l, s, g, units)
                        lru_stage1(l, s, g, bks)
                        queue.append(g)
                        if len(queue) > lag:
                            pg = queue.pop(0)
                            lru_stage2(l, s, pg, lru_gates(l, s, pg))
                        if l == 0 and slot == 0 and g + 1 < NG:
                            if STREAM0:
                                phase0_stream_group(g + 1)
                            else:
                                phase0_group(src_ap, src_bufs[g + 1], g + 1, (g + 1) % 2, gcol0, False)
                    deferred = [(l, s, pg) for pg in queue]
                    prefetch(wpos["n"] + RING)
                    pending2 = (l, s, ybase)
                else:
                    units = take_units(4)
                    prev_item = None
                    for g in range(NG):
                        bks = conv_main(l, s, g, units)
                        if prev_item is not None:
                            conv_stage(l, s, prev_item[0], prev_item[1], ybase)
                        prev_item = (g, bks)
                        if deferred:
                            dl, ds_, dg = deferred.pop(0)
                            lru_stage2(dl, ds_, dg, lru_gates(dl, ds_, dg))
                            if not deferred and pending2 is not None:
                                scan_pass(pending2[0], pending2[1], pending2[2], 1)
                                exchange(pending2[0], pending2[1])
                        if g == 4 and pending2 is not None:
                            scan_pass(pending2[0], pending2[1], pending2[2], 2)
                            pending2 = None
                    conv_stage(l, s, prev_item[0], prev_item[1], ybase)
                    prefetch(wpos["n"] + RING)
            phase2_v2(l, src_ap, src_bufs)
        S.final_wait(sp, B_out + B_out2 + ([B_spill[l][d][g] for l in range(NL) for d in range(KT) for g in range(NG)]
                                    if dbg else []) + dbg_bufs)

        block = es.enter_context(nc.Block())

        @block.tensor
        def _(e):
            for f in pe.prog:
                f(e)

        @block.scalar
        def _(e):
            for f in act.prog:
                f(e)

        @block.vector
        def _(e):
            for f in dve.prog:
                f(e)

        @block.gpsimd
        def _(e):
            for f in pool.prog:
                f(e)

        @block.sync
        def _(e):
            for f in sp.prog:
                f(e)
    return nc


def _tile_units(w, col_blocks):
    out = []
    for c0 in col_blocks:
        blk = w[:, c0:c0 + 128]
        out.append(blk.reshape(KT, 128, 128).transpose(1, 0, 2).reshape(128, KT * 128))
    return out


def _prep_shared(inp):
    f = np.float32
    w_in = np.asarray(inp["w_in"], f)
    w_out = np.asarray(inp["w_out"], f)
    wi_units, wo_units, wg_l = [], [], []
    for l in range(NL):
        cols = []
        for s in range(8):
            cols += [s * 128, 1024 + s * 128]
            cols += [2048 + s * 128, 3072 + s * 128, 4096 + s * 128, 5120 + s * 128]
        wi_units += _tile_units(w_in[l], cols)
        rows = []
        for s in range(8):
            rows += list(range(s * 128, (s + 1) * 128))
            rows += list(range(1024 + s * 128, 1024 + (s + 1) * 128))
        wp = w_out[l][rows, :]
        wo_units += _tile_units(wp, [d * 128 for d in range(16)])
        g = np.zeros((128, 8, 2, 128), f)
        for s in range(8):
            for gi, nm in enumerate(["lru_wr", "lru_wi"]):
                wgt = np.asarray(inp[nm], f)[l]
                for hh in range(2):
                    g[hh * 64:(hh + 1) * 64, s, gi, hh * 64:(hh + 1) * 64] = wgt[2 * s + hh]
        wg_l.append(g.reshape(128, 8 * 2 * 128))
    prm = np.zeros((128, NPRM), f)

    def pp(v, n):
        return np.asarray(v, f).reshape(n, 128).T

    for l in range(NL):
        prm[:, _pc(l, P_G):_pc(l, P_G) + 16] = pp(inp["norm_g"][l], 16)
        caw = np.asarray(inp["conv_a_w"], f)[l]
        for s in range(8):
            for k in range(4):
                prm[:, _pc(l, P_CAW) + s * 4 + k] = caw[k, s * 128:(s + 1) * 128]
        prm[:, _pc(l, P_CAB):_pc(l, P_CAB) + 8] = pp(inp["conv_a_b"][l], 8)
        prm[:, _pc(l, P_BR):_pc(l, P_BR) + 8] = pp(inp["lru_br"][l], 8)
        prm[:, _pc(l, P_BI):_pc(l, P_BI) + 8] = pp(inp["lru_bi"][l], 8)
        prm[:, _pc(l, P_LAM):_pc(l, P_LAM) + 8] = pp(inp["lru_lambda"][l], 8)
        cbw = np.asarray(inp["conv_b_w"], f)[l]
        for s in range(8):
            for k in range(3):
                prm[:, _pc(l, P_CBW) + s * 3 + k] = cbw[k, s * 128:(s + 1) * 128]
    prm[:, P_FG:P_FG + 16] = pp(inp["final_g"], 16)
    return (np.ascontiguousarray(np.concatenate(wi_units, 0)),
            np.ascontiguousarray(np.concatenate(wo_units, 0)),
            np.ascontiguousarray(np.concatenate(wg_l, 0)), prm)


def _make_in_maps(inp):
    f = np.float32
    x = np.asarray(inp["x"], f)
    meta = np.asarray(inp["meta"], f)
    wi, wo, wgm, prm = _prep_shared(inp)
    in_maps = []
    for c in range(8):
        b, half = c // 2, c % 2
        loc = np.zeros((T, D), f)
        tm = np.ones((128, 8), f)
        p = prm.copy()
        if half == 0:
            loc[NPRE:NPRE + NMETA] = meta
            loc[NPRE + NMETA:] = x[b, :SPLIT]
            tm[:, :NPRE] = 0.0
            p[:, P_FLAG] = 0.0
        else:
            loc[:] = x[b, SPLIT - NPRE:]
            p[:, P_FLAG] = 1.0
        in_maps.append({"xT": np.ascontiguousarray(loc.T), "w_in": wi, "w_out": wo, "wg": wgm,
                        "prm": p, "tmask": tm})
    return in_maps


def kernel(**inputs):
    in_maps = _make_in_maps(inputs)
    nc = build_program()
    res = run_bass_kernel_spmd(nc, in_maps, core_ids=list(range(8)))
    out = np.empty((4, SEQ, D), np.float32)
    for c in range(8):
        b, half = c // 2, c % 2
        og = res.results[c]["outT"]
        o = np.empty((D, T), np.float32)
        for off, n in GROUPS:
            blk = og[:, KT * off:KT * (off + n)].reshape(128, KT, n)
            o.reshape(KT, 128, T)[:, :, off:off + n] = blk.transpose(1, 0, 2)
        if half == 0:
            out[b, :SPLIT] = o[:, NPRE + NMETA:].T
        else:
            out[b, SPLIT:] = o[:, NPRE:].T
    return out
```

```python
import numpy as np
from contextlib import ExitStack

import concourse.bass as bass
import concourse.mybir as mybir
from concourse.bass_utils import run_bass_kernel_spmd

F32 = mybir.dt.float32
BF16 = mybir.dt.bfloat16
AF = mybir.ActivationFunctionType
ALU = mybir.AluOpType

D = 2048
KT = 16
NL = 2
SEQ = 4096
NMETA = 16
NPRE = 6
OWN = 2056
T = NPRE + OWN
SPLIT = OWN - NMETA
GROUPS = [(0, 413), (413, 413), (826, 412), (1238, 412), (1650, 412)]
NG = len(GROUPS)
GW = 416
RING = 6
STREAM0 = True
TAILD = 4
P2_PASSES = [[0, 1], [2, 3], [4]]
EPS = 1e-6

PL = 104
def _pc(l, off):
    return l * PL + off
P_G, P_CAW, P_CAB, P_BR, P_BI, P_LAM, P_CBW = 0, 16, 48, 56, 64, 72, 80
P_FG = NL * PL
P_FLAG = P_FG + 16
NPRM = P_FLAG + 1


SELF_SYNC = ("act", "dve")


class Tok:
    __slots__ = ("sem", "val", "key")

    def __init__(self, sem, val, key):
        self.sem, self.val, self.key = sem, val, key


class Buf:
    def __init__(self, name):
        self.name = name
        self.w = {}
        self.r = {}
        self.alias = []


class Eng:
    def __init__(self, name, sem):
        self.name, self.sem, self.count = name, sem, 0
        self.waited = {}
        self.prog = []


class DSem:
    def __init__(self, sem, key):
        self.sem, self.key, self.count = sem, key, 0


class Sched:
    def __init__(self):
        self.engs = {}

    def _collect(self, eng, reads, writes):
        need = {}

        def add(d):
            for key, tok in d.items():
                if key == eng.name and eng.name not in SELF_SYNC:
                    continue
                if key not in need or need[key].val < tok.val:
                    need[key] = tok

        for b in reads:
            for bb in [b] + b.alias:
                add(bb.w)
        for b in writes:
            for bb in [b] + b.alias:
                add(bb.w)
                add(bb.r)
        out = []
        for key, tok in need.items():
            if eng.waited.get(key, 0) >= tok.val:
                continue
            eng.waited[key] = tok.val
            out.append(tok)
        return out

    def _update(self, tok, reads, writes):
        for b in reads:
            old = b.r.get(tok.key)
            if old is None or old.val < tok.val:
                b.r[tok.key] = tok
        for b in writes:
            b.w = {tok.key: tok}
            b.r = {}

    def op(self, eng, fn, reads=(), writes=()):
        waits = self._collect(eng, reads, writes)
        eng.count += 1
        tok = Tok(eng.sem, eng.count, eng.name)

        def run(e, waits=waits, fn=fn, sem=eng.sem):
            for w in waits:
                e.wait_ge(w.sem, w.val)
            ins = fn(e)
            ins.then_inc(sem, 1)

        eng.prog.append(run)
        self._update(tok, reads, writes)
        return tok

    def dma(self, eng, dsem, fn, reads=(), writes=()):
        waits = self._collect(eng, reads, writes)
        dsem.count += 16
        tok = Tok(dsem.sem, dsem.count, dsem.key)

        def run(e, waits=waits, fn=fn, sem=dsem.sem):
            for w in waits:
                e.wait_ge(w.sem, w.val)
            fn(e).then_inc(sem, 16)

        eng.prog.append(run)
        self._update(tok, reads, writes)
        return tok

    def coll(self, eng, dsem, fn, reads=(), writes=()):
        waits = self._collect(eng, reads, writes)
        dsem.count += 1
        tok = Tok(dsem.sem, dsem.count, dsem.key)

        def run(e, waits=waits, fn=fn, sem=dsem.sem):
            for w in waits:
                e.wait_ge(w.sem, w.val)
            fn(e).then_inc(sem)

        eng.prog.append(run)
        self._update(tok, reads, writes)
        return tok

    def final_wait(self, eng, bufs):
        waits = self._collect(eng, [], bufs)

        def run(e, waits=waits):
            for w in waits:
                e.wait_ge(w.sem, w.val)

        eng.prog.append(run)


def build_program(dbg=False):
    nc = bass.Bass("TRN2", target_bir_lowering=False)
    xT = nc.dram_tensor("xT", [D, T], F32, kind="ExternalInput").ap()
    w_in = nc.dram_tensor("w_in", [NL * 48 * 128, KT * 128], F32, kind="ExternalInput").ap()
    w_out = nc.dram_tensor("w_out", [NL * 16 * 128, KT * 128], F32, kind="ExternalInput").ap()
    wg = nc.dram_tensor("wg", [NL * 128, 8 * 2 * 128], F32, kind="ExternalInput").ap()
    prm = nc.dram_tensor("prm", [128, NPRM], F32, kind="ExternalInput").ap()
    tmask = nc.dram_tensor("tmask", [128, 8], F32, kind="ExternalInput").ap()
    outT = nc.dram_tensor("outT", [128, KT * T], F32, kind="ExternalOutput").ap()
    spill = []
    for l in range(NL):
        if dbg:
            spill.append(nc.dram_tensor(f"spill{l}", [D, T], F32, kind="ExternalOutput").ap())
        else:
            spill.append(nc.dram_tensor(f"spill{l}", [D, T], F32).ap())
    if dbg:
        dbg_drv = nc.dram_tensor("dbg_drv", [128, NL * 32], F32, kind="ExternalOutput").ap()
        dbg_ab = nc.dram_tensor("dbg_ab", [128, 3 * T], F32, kind="ExternalOutput").ap()
        dbg_st = nc.dram_tensor("dbg_st", [128, 6], F32, kind="ExternalOutput").ap()
    bnc_in = [[nc.dram_tensor(f"bi_{l}_{s}", [128, 1], F32) for s in range(8)] for l in range(NL)]
    bnc_out = [[nc.dram_tensor(f"bo_{l}_{s}", [256, 1], F32) for s in range(8)] for l in range(NL)]

    S = Sched()
    with ExitStack() as es:
        def sb(name, shape, dt):
            return es.enter_context(nc.sbuf_tensor(name, shape, dt))

        def sem(name):
            return es.enter_context(nc.semaphore(name))

        bufA = sb("bufA", [128, KT * T], BF16)
        bufB = sb("bufB", [128, KT * T], BF16)
        LRUW = max(3 * T, KT * GW)
        lru_big = sb("lru_big", [128, LRUW], F32)
        a_full = lru_big[:, 0:T]
        b_full = lru_big[:, T:2 * T]
        t1_full = lru_big[:, 2 * T:3 * T]
        ring = [sb(f"ring{i}", [128, KT * 128], BF16) for i in range(RING)]
        NTMP = 9
        tmp = [sb(f"tmp{i}", [128, GW], F32) for i in range(NTMP)]
        halo = [sb(f"halo{i}", [128, GW + 4], F32) for i in range(4)]
        wgs = sb("wgs", [128, 8 * 2 * 128], BF16)
        prm_sb = sb("prm_sb", [128, NPRM], F32)
        drv = sb("drv", [128, NL * 32], F32)
        sc = sb("scr", [128, 64], F32)
        tm_sb = sb("tm_sb", [128, 8], F32)
        ones = sb("ones", [128, 128], BF16)
        epsb = sb("epsb", [128, 1], F32)
        stx = sb("stx", [128, 2], F32)
        gath = sb("gath", [128, 2], F32)
        init = sb("init", [128, 2], F32)
        banks = [es.enter_context(nc.psum_tensor(f"bank{i}", [128, 512], F32)) for i in range(8)]

        E = {}
        for nm in ["pe", "act", "dve", "pool", "sp"]:
            E[nm] = Eng(nm, sem("s_" + nm))
        pe, act, dve, pool, sp = E["pe"], E["act"], E["dve"], E["pool"], E["sp"]

        def dsem(name):
            return DSem(sem("d_" + name), "d_" + name)

        ring_ds = [dsem(f"ring{i}") for i in range(RING)]
        tmp_ds = [dsem(f"tmp{i}") for i in range(NTMP)]
        tmp_st_ds = [dsem(f"tst{i}") for i in range(NTMP)]
        wgs_ds = dsem("wgs")
        stage_ds = [dsem("stage0"), dsem("stage1"), dsem("stage2"), dsem("stage3"), dsem("stage4")]
        misc_ds = dsem("misc")
        st_ds = dsem("stx")
        ga_ds = dsem("gath")
        cc_ds = dsem("cc")

        B_ring = [Buf(f"ring{i}") for i in range(RING)]
        B_tmp = [Buf(f"tmp{i}") for i in range(NTMP)]
        B_halo = [Buf(f"halo{i}") for i in range(4)]
        B_bank = [Buf(f"bank{i}") for i in range(8)]
        B_hn = [Buf(f"hn{g}") for g in range(NG)]
        B_y = [[Buf(f"y{k}_{g}") for g in range(NG)] for k in range(KT)]
        B_stage = [Buf("stage0"), Buf("stage1"), Buf("stage2"), Buf("stage3"), Buf("stage4")]
        for si, ks in enumerate([range(2, 9), range(9, 16)]):
            for k in ks:
                for g in range(NG):
                    B_stage[si].alias.append(B_y[k][g])
                    B_y[k][g].alias.append(B_stage[si])
        B_a = [Buf(f"a{g}") for g in range(NG)]
        B_b = [Buf(f"b{g}") for g in range(NG)]
        B_t1 = [Buf(f"t1{g}") for g in range(NG)]
        B_wgs = Buf("wgs")
        B_prm = Buf("prm")
        B_drv = Buf("drv")
        B_const = Buf("const")
        B_stx = [Buf("stx0"), Buf("stx1")]
        B_gath = [Buf("gath0"), Buf("gath1")]
        B_init = [Buf("init0"), Buf("init1")]
        B_spill = [[[Buf(f"sp{l}_{d}_{g}") for g in range(NG)] for d in range(KT)] for l in range(NL)]
        B_out = [Buf(f"outT{g}") for g in range(NG)]
        B_bin = [[Buf(f"bin{l}_{s}") for s in range(8)] for l in range(NL)]
        B_bout = [[Buf(f"bout{l}_{s}") for s in range(8)] for l in range(NL)]

        for g in range(NG):
            for bb in (B_a[g], B_b[g], B_t1[g]):
                B_stage[2].alias.append(bb)
                bb.alias.append(B_stage[2])
            for si in (3, 4):
                B_stage[si].alias.append(B_hn[g])
                B_hn[g].alias.append(B_stage[si])
        SW = KT * GW
        stage_ap = [bufB[:, 2 * T: 9 * T].bitcast(F32), bufB[:, 9 * T: 16 * T].bitcast(F32),
                    lru_big[:, 0:SW],
                    bufA[:, 0: 2 * SW].bitcast(F32), bufA[:, 2 * SW: 4 * SW].bitcast(F32)]

        def stage_k(si, k, n):
            return stage_ap[si][:, k * GW: k * GW + n]

        rot = {"tmp": 0, "bank": 0, "ring": 0}

        def new_tmp():
            i = rot["tmp"]
            rot["tmp"] = (i + 1) % NTMP
            return i

        bank_free = list(range(8))

        def new_bank():
            assert bank_free, "no free PSUM bank at this point of the schedule"
            return bank_free.pop(0)

        def rel_bank(*bs):
            for b in bs:
                assert b not in bank_free
                bank_free.append(b)

        def take_bank(b):
            assert b in bank_free, f"PSUM bank {b} still live"
            bank_free.remove(b)
            return b

        def P(col, n=1):
            return prm_sb[:, col: col + n]

        wunits = []
        for l in range(NL):
            for u in range(48):
                r0 = (l * 48 + u) * 128
                wunits.append(w_in[r0: r0 + 128, :])
            for d in range(16):
                r0 = (l * 16 + d) * 128
                wunits.append(w_out[r0: r0 + 128, :])
        wstate = {"issued": 0}
        unit_slot = {}

        def prefetch(upto, extra_reads=()):
            upto = min(upto, len(wunits))
            while wstate["issued"] < upto:
                n = wstate["issued"]
                slot = n % RING
                unit_slot[n] = slot
                src = wunits[n]
                S.dma(pool, ring_ds[slot],
                      lambda e, slot=slot, src=src: e.dma_start(out=ring[slot][:, :], in_=src),
                      reads=list(extra_reads), writes=[B_ring[slot]])
                wstate["issued"] += 1

        wpos = {"n": 0}

        def take_units(n):
            base = wpos["n"]
            wpos["n"] += n
            prefetch(base + n)
            return [unit_slot[base + i] for i in range(n)]

        S.dma(sp, misc_ds, lambda e: e.dma_start(out=prm_sb[:, :], in_=prm), writes=[B_prm])
        S.dma(sp, misc_ds, lambda e: e.dma_start(out=tm_sb[:, :], in_=tmask), writes=[B_prm])
        S.op(dve, lambda e: e.memset(epsb[:, :], EPS), writes=[B_const])
        S.op(dve, lambda e: e.memset(ones[:, :], 1.0), writes=[B_const])
        for i in range(4):
            S.op(dve, lambda e, i=i: e.memset(halo[i][:, :], 0.0), writes=[B_halo[i]])

        def derive():
            for l in range(NL):
                derive_layer(l)

        def derive_layer(l):
            if True:
                lam = P(_pc(l, P_LAM), 8)
                o = l * 32
                ee, uu, dd, lnu, rr, zz = (sc[:, 0:8], sc[:, 8:16], sc[:, 16:24], sc[:, 24:32],
                                           sc[:, 32:40], sc[:, 40:48])
                S.op(act, lambda e: e.activation(out=ee, in_=lam, func=AF.Exp, scale=-1.0),
                     reads=[B_prm], writes=[B_drv])
                S.op(dve, lambda e: e.tensor_scalar(out=uu, in0=ee, scalar1=1.0, scalar2=None, op0=ALU.add),
                     reads=[B_drv], writes=[B_drv])
                S.op(dve, lambda e: e.tensor_scalar(out=dd, in0=uu, scalar1=-1.0, scalar2=None, op0=ALU.add),
                     writes=[B_drv])
                S.op(dve, lambda e: e.tensor_scalar(out=zz, in0=dd, scalar1=0.0, scalar2=None, op0=ALU.is_equal),
                     writes=[B_drv])
                S.op(dve, lambda e: e.tensor_scalar(out=dd, in0=dd, scalar1=1e-30, scalar2=None, op0=ALU.max),
                     writes=[B_drv])
                S.op(dve, lambda e: e.reciprocal(out=rr, in_=dd), writes=[B_drv])
                S.op(act, lambda e: e.activation(out=lnu, in_=uu, func=AF.Ln), reads=[B_drv], writes=[B_drv])
                S.op(dve, lambda e: e.tensor_tensor(out=rr, in0=rr, in1=lnu, op=ALU.mult),
                     reads=[B_drv], writes=[B_drv])
                S.op(dve, lambda e: e.tensor_tensor(out=rr, in0=rr, in1=zz, op=ALU.add), writes=[B_drv])
                S.op(dve, lambda e: e.tensor_tensor(out=rr, in0=rr, in1=ee, op=ALU.mult), writes=[B_drv])
                S.op(dve, lambda e, o=o: e.tensor_scalar(out=drv[:, o + 16: o + 24], in0=rr, scalar1=-8.0,
                                                         scalar2=None, op0=ALU.mult), writes=[B_drv])
                S.op(dve, lambda e, o=o: e.tensor_scalar(out=drv[:, o + 24: o + 32], in0=rr, scalar1=-4.0,
                                                         scalar2=None, op0=ALU.mult), writes=[B_drv])
                S.op(dve, lambda e, o=o, l=l: e.tensor_scalar(out=drv[:, o: o + 8], in0=P(_pc(l, P_BR), 8),
                                                              scalar1=0.5, scalar2=None, op0=ALU.mult),
                     writes=[B_drv])
                S.op(dve, lambda e, o=o, l=l: e.tensor_scalar(out=drv[:, o + 8: o + 16], in0=P(_pc(l, P_BI), 8),
                                                              scalar1=0.5, scalar2=None, op0=ALU.mult),
                     writes=[B_drv])

        derive()

        def DV(l, which, s):
            o = l * 32 + which * 8 + s
            return drv[:, o: o + 1]

        def phase0_load(src_ap, src_bufs, g, si):
            off, n = GROUPS[g]
            S.dma(sp, stage_ds[si],
                  lambda e: e.dma_start(
                      out=stage_ap[si][:, 0: KT * GW].rearrange("p (k w) -> p k w", k=KT)[:, :, 0:n],
                      in_=src_ap.rearrange("(k p) t -> p k t", p=128)[:, :, off: off + n]),
                  reads=src_bufs, writes=[B_stage[si]])

        def phase0_group(src_ap, src_bufs, g, si, gcol0, final):
            phase0_load(src_ap, src_bufs, g, si)
            phase0_compute(g, si, gcol0, final)

        def phase0_compute(g, si, gcol0, final):
            off, n = GROUPS[g]
            bk = new_bank()
            for k in range(KT):
                ti = new_tmp()
                S.op(act, lambda e, k=k, ti=ti: e.activation(out=tmp[ti][:, 0:n].bitcast(BF16)[:, 0:n],
                                                             in_=stage_k(si, k, n), func=AF.Square),
                     reads=[B_stage[si]], writes=[B_tmp[ti]])
                S.op(pe, lambda e, k=k, ti=ti: e.matmul(banks[bk][:, 0:n], lhsT=ones[:, :],
                                                        rhs=tmp[ti][:, 0:n].bitcast(BF16)[:, 0:n],
                                                        start=(k == 0), stop=(k == KT - 1)),
                     reads=[B_tmp[ti], B_const], writes=[B_bank[bk]])
            ti = new_tmp()
            S.op(act, lambda e: e.activation(out=tmp[ti][:, 0:n], in_=banks[bk][:, 0:n], func=AF.Sqrt,
                                             bias=epsb[:, 0:1], scale=1.0 / D),
                 reads=[B_bank[bk], B_const], writes=[B_tmp[ti]])
            S.op(dve, lambda e: e.reciprocal(out=banks[bk][:, 0:n], in_=tmp[ti][:, 0:n]),
                 reads=[B_tmp[ti]], writes=[B_bank[bk]])
            for k in range(KT):
                if not final:
                    S.op(dve, lambda e, k=k: e.scalar_tensor_tensor(
                        out=bufA[:, k * T + off: k * T + off + n], in0=stage_k(si, k, n),
                        scalar=P(gcol0 + k), in1=banks[bk][:, 0:n], op0=ALU.mult, op1=ALU.mult),
                        reads=[B_stage[si], B_bank[bk], B_prm], writes=[B_hn[g]])
                else:
                    S.op(dve, lambda e, k=k: e.scalar_tensor_tensor(
                        out=stage_k(si, k, n), in0=stage_k(si, k, n),
                        scalar=P(gcol0 + k), in1=banks[bk][:, 0:n], op0=ALU.mult, op1=ALU.mult),
                        reads=[B_bank[bk], B_prm], writes=[B_stage[si]])
            if final:
                S.dma(pool, stage_ds[si],
                      lambda e: e.dma_start(
                          out=outT.rearrange("(k p) t -> p k t", p=128)[:, :, off: off + n],
                          in_=stage_ap[si][:, 0: KT * GW].rearrange("p (k w) -> p k w", k=KT)[:, :, 0:n]),
                      reads=[B_stage[si]], writes=[B_out[g]])
            rel_bank(bk)

        def mm_group(bk, unit, g):
            off, n = GROUPS[g]

            def fn(e):
                ins = None
                for k in range(KT):
                    ins = e.matmul(banks[bk][:, 0:n], lhsT=ring[unit][:, k * 128:(k + 1) * 128],
                                   rhs=bufA[:, k * T + off: k * T + off + n],
                                   start=(k == 0), stop=(k == KT - 1))
                return ins
            S.op(pe, fn, reads=[B_ring[unit], B_hn[g]], writes=[B_bank[bk]])

        def lru_main(l, s, g, units):
            bks = [new_bank(), new_bank()]
            mm_group(bks[0], units[0], g)
            mm_group(bks[1], units[1], g)
            return bks

        def xcb_view(off, n):
            return lru_big[:, off:off + n].bitcast(BF16)[:, 0:n]

        def lru_stage1(l, s, g, bks):
            off, n = GROUPS[g]
            hb = halo[g % 2]
            hbn = halo[(g + 1) % 2]
            Bh, Bhn = B_halo[g % 2], B_halo[(g + 1) % 2]
            xa_ps, ga_ps = banks[bks[0]], banks[bks[1]]
            Bxa, Bga = B_bank[bks[0]], B_bank[bks[1]]
            cw = _pc(l, P_CAW) + s * 4
            if g == 0:
                S.op(dve, lambda e: e.memset(hb[:, 0:3], 0.0), writes=[Bh])
            S.op(act, lambda e: e.activation(out=hb[:, 3:3 + n], in_=xa_ps[:, 0:n], func=AF.Identity),
                 reads=[Bxa], writes=[Bh])
            S.op(act, lambda e: e.activation(out=xa_ps[:, 0:n], in_=xa_ps[:, 0:n], func=AF.Identity,
                                             scale=P(cw + 3), bias=P(_pc(l, P_CAB) + s)),
                 reads=[B_prm], writes=[Bxa])
            thg = new_tmp()
            S.op(act, lambda e: e.activation(out=tmp[thg][:, 0:n], in_=ga_ps[:, 0:n], func=AF.Tanh, scale=0.5),
                 reads=[Bga], writes=[B_tmp[thg]])
            if g + 1 < NG:
                S.op(act, lambda e: e.activation(out=hbn[:, 0:3], in_=hb[:, n:n + 3], func=AF.Identity),
                     reads=[Bh], writes=[Bhn])
            for tap in (2, 1):
                S.op(dve, lambda e, tap=tap: e.scalar_tensor_tensor(
                    out=xa_ps[:, 0:n], in0=hb[:, tap:tap + n], scalar=P(cw + tap), in1=xa_ps[:, 0:n],
                    op0=ALU.mult, op1=ALU.add),
                    reads=[Bh, B_prm], writes=[Bxa])
            S.op(dve, lambda e: e.scalar_tensor_tensor(
                out=b_full[:, off:off + n], in0=hb[:, 0:n], scalar=P(cw + 0), in1=xa_ps[:, 0:n],
                op0=ALU.mult, op1=ALU.add),
                reads=[Bh, Bxa, B_prm], writes=[B_b[g]])
            S.op(dve, lambda e: e.scalar_tensor_tensor(
                out=t1_full[:, off:off + n], in0=tmp[thg][:, 0:n], scalar=1.0, in1=ga_ps[:, 0:n],
                op0=ALU.add, op1=ALU.mult),
                reads=[B_tmp[thg], Bga], writes=[B_t1[g]])
            S.op(act, lambda e: e.activation(out=xcb_view(off, n), in_=b_full[:, off:off + n],
                                             func=AF.Identity),
                 reads=[B_b[g]], writes=[B_a[g]])
            rel_bank(*bks)

        def lru_gates(l, s, g):
            off, n = GROUPS[g]
            gb2 = [new_bank(), new_bank()]
            for gi in range(2):
                S.op(pe, lambda e, gi=gi: e.matmul(
                    banks[gb2[gi]][:, 0:n], lhsT=wgs[:, (s * 2 + gi) * 128:(s * 2 + gi + 1) * 128],
                    rhs=xcb_view(off, n), start=True, stop=True),
                    reads=[B_wgs, B_a[g]], writes=[B_bank[gb2[gi]]])
            return gb2

        def lru_stage2(l, s, g, gb2):
            off, n = GROUPS[g]
            r_ps, i_ps = banks[gb2[0]], banks[gb2[1]]
            Br, Bi = B_bank[gb2[0]], B_bank[gb2[1]]
            t_r, t_i = new_tmp(), new_tmp()
            S.op(act, lambda e: e.activation(out=tmp[t_r][:, 0:n], in_=r_ps[:, 0:n], func=AF.Tanh,
                                             scale=0.5, bias=DV(l, 0, s)),
                 reads=[Br, B_drv], writes=[B_tmp[t_r]])
            S.op(act, lambda e: e.activation(out=tmp[t_i][:, 0:n], in_=i_ps[:, 0:n], func=AF.Tanh,
                                             scale=0.5, bias=DV(l, 1, s)),
                 reads=[Bi, B_drv], writes=[B_tmp[t_i]])
            S.op(act, lambda e: e.activation(out=a_full[:, off:off + n], in_=tmp[t_r][:, 0:n], func=AF.Exp,
                                             scale=DV(l, 3, s), bias=DV(l, 3, s)),
                 reads=[B_tmp[t_r], B_drv], writes=[B_a[g]])
            S.op(act, lambda e: e.activation(out=tmp[t_r][:, 0:n], in_=tmp[t_r][:, 0:n], func=AF.Exp,
                                             scale=DV(l, 2, s), bias=DV(l, 2, s)),
                 reads=[B_drv], writes=[B_tmp[t_r]])
            S.op(act, lambda e: e.activation(out=tmp[t_r][:, 0:n], in_=tmp[t_r][:, 0:n], func=AF.Sqrt,
                                             scale=-0.25, bias=0.25),
                 reads=[], writes=[B_tmp[t_r]])
            S.op(dve, lambda e: e.scalar_tensor_tensor(
                out=tmp[t_i][:, 0:n], in0=tmp[t_i][:, 0:n], scalar=1.0, in1=b_full[:, off:off + n],
                op0=ALU.add, op1=ALU.mult),
                reads=[B_b[g]], writes=[B_tmp[t_i]])
            S.op(dve, lambda e: e.tensor_tensor(out=b_full[:, off:off + n], in0=tmp[t_i][:, 0:n],
                                                in1=tmp[t_r][:, 0:n], op=ALU.mult),
                 reads=[B_tmp[t_i], B_tmp[t_r]], writes=[B_b[g]])
            if g == 0:
                S.op(dve, lambda e: e.tensor_tensor(out=b_full[:, 0:8], in0=b_full[:, 0:8], in1=tm_sb[:, 0:8],
                                                    op=ALU.mult),
                     reads=[B_prm], writes=[B_b[g]])
            rel_bank(*gb2)

        def scan_pass(l, s, ybase, which):
            s0 = 3 * (l + 1)
            par = s % 2
            prev = None
            if which == 2:
                S.op(dve, lambda e: e.tensor_tensor(out=init[:, par:par + 1], in0=gath[:, par:par + 1],
                                                    in1=P(P_FLAG), op=ALU.mult),
                     reads=[B_gath[par], B_prm], writes=[B_init[par]])
            for g in range(NG):
                off, n = GROUPS[g]
                lo = s0 if g == 0 else 0
                scr = new_tmp()
                if g == 0:
                    if which == 2:
                        S.op(dve, lambda e, scr=scr, lo=lo: e.memset(tmp[scr][:, 0:lo], 0.0), writes=[B_tmp[scr]])
                        ini = init[:, par:par + 1]
                        inib = [B_init[par]]
                    else:
                        ini = 0.0
                        inib = []
                else:
                    pn = GROUPS[g - 1][1]
                    ini = tmp[prev][:, pn - 1:pn]
                    inib = [B_tmp[prev]]
                S.op(dve, lambda e, scr=scr, ini=ini, lo=lo, off=off, n=n: e.tensor_tensor_scan(
                    out=tmp[scr][:, lo:n], data0=a_full[:, off + lo:off + n], data1=b_full[:, off + lo:off + n],
                    initial=ini, op0=ALU.mult, op1=ALU.add),
                    reads=[B_a[g], B_b[g]] + inib, writes=[B_tmp[scr]])
                if which == 2:
                    S.op(dve, lambda e, scr=scr, off=off, n=n: e.scalar_tensor_tensor(
                        out=bufB[:, ybase * T + off: ybase * T + off + n], in0=tmp[scr][:, 0:n], scalar=0.5,
                        in1=t1_full[:, off:off + n], op0=ALU.mult, op1=ALU.mult),
                        reads=[B_tmp[scr], B_t1[g]], writes=[B_y[ybase][g]])
                prev = scr
            if which == 1:
                n = GROUPS[NG - 1][1]
                col = n - 4 if l == 0 else n - 1
                S.op(dve, lambda e: e.tensor_copy(out=stx[:, par:par + 1], in_=tmp[prev][:, col:col + 1]),
                     reads=[B_tmp[prev]], writes=[B_stx[par]])
                S.dma(sp, st_ds, lambda e: e.dma_start(out=bnc_in[l][s].ap(), in_=stx[:, par:par + 1]),
                      reads=[B_stx[par]], writes=[B_bin[l][s]])

        def exchange(l, s):
            par = s % 2
            S.coll(pool, cc_ds, lambda e: e.collective_compute(
                "AllGather", ALU.bypass, replica_groups=[[0, 1], [2, 3], [4, 5], [6, 7]],
                ins=[bnc_in[l][s].ap().opt()], outs=[bnc_out[l][s].ap().opt()]),
                reads=[B_bin[l][s]], writes=[B_bout[l][s]])
            S.dma(sp, ga_ds, lambda e: e.dma_start(out=gath[:, par:par + 1], in_=bnc_out[l][s].ap()[0:128, :]),
                  reads=[B_bout[l][s]], writes=[B_gath[par]])

        def conv_main(l, s, g, units):
            bks = [new_bank() for _ in range(4)]
            for j in range(4):
                mm_group(bks[j], units[j], g)
            return bks

        def conv_stage(l, s, g, bks, ybase):
            off, n = GROUPS[g]
            hb = halo[2 + g % 2]
            hbn = halo[2 + (g + 1) % 2]
            Bh, Bhn = B_halo[2 + g % 2], B_halo[2 + (g + 1) % 2]
            gb_ps_, gc_ps, xb_ps, gbk = banks[bks[0]], banks[bks[1]], banks[bks[2]], banks[bks[3]]
            Bgb, Bgc, Bxb, Bgk = B_bank[bks[0]], B_bank[bks[1]], B_bank[bks[2]], B_bank[bks[3]]
            cw = _pc(l, P_CBW) + s * 3
            if g == 0:
                S.op(dve, lambda e: e.memset(hb[:, 0:2], 0.0), writes=[Bh])
            xb = new_tmp()
            S.op(act, lambda e: e.activation(out=tmp[xb][:, 0:n], in_=xb_ps[:, 0:n], func=AF.Identity),
                 reads=[Bxb], writes=[B_tmp[xb]])
            thg = new_tmp()
            S.op(act, lambda e: e.activation(out=tmp[thg][:, 0:n], in_=gbk[:, 0:n], func=AF.Tanh, scale=0.5),
                 reads=[Bgk], writes=[B_tmp[thg]])
            S.op(dve, lambda e: e.tensor_tensor(out=hb[:, 2:2 + n], in0=tmp[xb][:, 0:n], in1=gc_ps[:, 0:n],
                                                op=ALU.mult),
                 reads=[B_tmp[xb], Bgc], writes=[Bh])
            S.op(act, lambda e: e.activation(out=gc_ps[:, 0:n], in_=hb[:, 2:2 + n], func=AF.Identity,
                                             scale=P(cw + 2)),
                 reads=[Bh, B_prm], writes=[Bgc])
            if g + 1 < NG:
                S.op(act, lambda e: e.activation(out=hbn[:, 0:2], in_=hb[:, n:n + 2], func=AF.Identity),
                     reads=[Bh], writes=[Bhn])
            S.op(dve, lambda e: e.scalar_tensor_tensor(
                out=gc_ps[:, 0:n], in0=hb[:, 1:1 + n], scalar=P(cw + 1), in1=gc_ps[:, 0:n],
                op0=ALU.mult, op1=ALU.add),
                reads=[Bh, B_prm], writes=[Bgc])
            q2 = new_tmp()
            S.op(dve, lambda e: e.scalar_tensor_tensor(
                out=tmp[q2][:, 0:n], in0=hb[:, 0:n], scalar=P(cw + 0), in1=gc_ps[:, 0:n],
                op0=ALU.mult, op1=ALU.add),
                reads=[Bh, Bgc, B_prm], writes=[B_tmp[q2]])
            S.op(dve, lambda e: e.tensor_tensor(out=tmp[q2][:, 0:n], in0=tmp[q2][:, 0:n], in1=gb_ps_[:, 0:n],
                                                op=ALU.mult),
                 reads=[Bgb], writes=[B_tmp[q2]])
            S.op(dve, lambda e: e.scalar_tensor_tensor(
                out=xb_ps[:, 0:n], in0=tmp[thg][:, 0:n], scalar=1.0, in1=gbk[:, 0:n],
                op0=ALU.add, op1=ALU.mult),
                reads=[B_tmp[thg], Bgk], writes=[Bxb])
            S.op(dve, lambda e: e.scalar_tensor_tensor(
                out=bufB[:, ybase * T + off: ybase * T + off + n], in0=tmp[q2][:, 0:n], scalar=0.5,
                in1=xb_ps[:, 0:n], op0=ALU.mult, op1=ALU.mult),
                reads=[B_tmp[q2], Bxb], writes=[B_y[ybase][g]])
            rel_bank(*bks)

        def phase2_item(l, d, g, unit, src_ap, src_buf):
            off, n = GROUPS[g]
            hin = new_tmp()
            S.dma(sp, tmp_ds[hin],
                  lambda e: e.dma_start(out=tmp[hin][:, 0:n], in_=src_ap[d * 128:(d + 1) * 128, off:off + n]),
                  reads=src_buf, writes=[B_tmp[hin]])
            bk = new_bank()

            def fn(e):
                ins = None
                for k in range(KT):
                    ins = e.matmul(banks[bk][:, 0:n], lhsT=ring[unit][:, k * 128:(k + 1) * 128],
                                   rhs=bufB[:, k * T + off: k * T + off + n],
                                   start=(k == 0), stop=(k == KT - 1))
                return ins
            S.op(pe, fn, reads=[B_ring[unit]] + [B_y[k][g] for k in range(KT)], writes=[B_bank[bk]])
            S.op(dve, lambda e: e.tensor_tensor(out=tmp[hin][:, 0:n], in0=tmp[hin][:, 0:n],
                                                in1=banks[bk][:, 0:n], op=ALU.add),
                 reads=[B_bank[bk]], writes=[B_tmp[hin]])
            S.dma(pool, tmp_st_ds[hin],
                  lambda e: e.dma_start(out=spill[l][d * 128:(d + 1) * 128, off:off + n], in_=tmp[hin][:, 0:n]),
                  reads=[B_tmp[hin]], writes=[B_spill[l][d][g]])

        bufA_f32 = bufA[:, :].bitcast(F32)
        wgs_f32 = wgs[:, :].bitcast(F32)
        NHOLD = KT - TAILD
        B_hold = [[Buf(f"hold{d}_{g}") for g in range(NG)] for d in range(NHOLD)]
        for d in range(NHOLD):
            for g in range(NG):
                if d < 8:
                    others = B_hn
                elif d < 11:
                    others = B_a + B_b + B_t1
                else:
                    others = [B_halo[g]] if g < 4 else [B_wgs]
                for ob in others:
                    B_hold[d][g].alias.append(ob)
                    ob.alias.append(B_hold[d][g])
        B_out2 = []
        out_ds = dsem("outs")

        def hold_ap(d, g, off, n):
            if d < 8:
                return bufA_f32[:, 8 * off + d * n:8 * off + (d + 1) * n]
            if d < 11:
                return lru_big[:, 3 * off + (d - 8) * n:3 * off + (d - 7) * n]
            if g < 4:
                return halo[g][:, 0:n]
            return wgs_f32[:, 0:n]

        def out_ap(g, d0, d1):
            off, n = GROUPS[g]
            return outT[:, KT * off + d0 * n:KT * off + d1 * n]

        def phase2_v2(l, src_ap, src_bufs):
            last = (l == NL - 1)
            SQ = [take_bank(b) for b in (0, 1, 2, 3, 4)]
            DL = [take_bank(b) for b in (5, 6, 7)]
            st = {"dl": 0}
            gnext = P_FG if last else _pc(l + 1, P_G)
            pend = []
            tail_tmps = [[] for _ in range(NG)]
            free_t = list(range(NTMP))
            free_h = []

            def t_alloc():
                if not last:
                    return new_tmp()
                assert free_t, "no free temp"
                return free_t.pop(0)

            def t_free(i):
                if last:
                    free_t.append(i)

            def h_alloc():
                if not last:
                    return (new_tmp(), 0)
                if not free_h:
                    i = t_alloc()
                    free_h.extend([(i, 0), (i, 1)])
                return free_h.pop(0)

            def h_free(x):
                if last:
                    free_h.append(x)

            def sq_ap(x, n):
                i, h = x
                return tmp[i][:, h * (GW // 2):(h + 1) * (GW // 2)].bitcast(BF16)[:, 0:n]

            def flush_one():
                d, g, sq = pend.pop(0)
                off, n = GROUPS[g]
                S.op(pe, lambda e: e.matmul(banks[SQ[g]][:, 0:n], lhsT=ones[:, :],
                                            rhs=sq_ap(sq, n),
                                            start=(d == 0), stop=(d == KT - 1)),
                     reads=[B_tmp[sq[0]], B_const], writes=[B_bank[SQ[g]]])
                h_free(sq)

            def item(d, g, unit):
                off, n = GROUPS[g]
                hin = t_alloc()
                S.dma(sp, tmp_ds[hin],
                      lambda e: e.dma_start(out=tmp[hin][:, 0:n], in_=src_ap[d * 128:(d + 1) * 128, off:off + n]),
                      reads=(src_bufs[g] if l > 0 else []), writes=[B_tmp[hin]])
                bk = DL[st["dl"] % len(DL)]
                st["dl"] += 1

                def fn(e):
                    ins = None
                    for k in range(KT):
                        ins = e.matmul(banks[bk][:, 0:n], lhsT=ring[unit][:, k * 128:(k + 1) * 128],
                                       rhs=bufB[:, k * T + off: k * T + off + n],
                                       start=(k == 0), stop=(k == KT - 1))
                    return ins
                S.op(pe, fn, reads=[B_ring[unit]] + [B_y[k][g] for k in range(KT)], writes=[B_bank[bk]])
                if len(pend) >= 2:
                    flush_one()
                S.op(dve, lambda e: e.tensor_tensor(out=tmp[hin][:, 0:n], in0=tmp[hin][:, 0:n],
                                                    in1=banks[bk][:, 0:n], op=ALU.add),
                     reads=[B_bank[bk]], writes=[B_tmp[hin]])
                sq = h_alloc()
                S.op(act, lambda e: e.activation(out=sq_ap(sq, n), in_=tmp[hin][:, 0:n],
                                                 func=AF.Square),
                     reads=[B_tmp[hin]], writes=[B_tmp[sq[0]]])
                pend.append((d, g, sq))
                if not last:
                    S.dma(pool, tmp_st_ds[hin],
                          lambda e: e.dma_start(out=spill[l][d * 128:(d + 1) * 128, off:off + n],
                                                in_=tmp[hin][:, 0:n]),
                          reads=[B_tmp[hin]], writes=[B_spill[l][d][g]])
                    S.op(act, lambda e: e.activation(out=bufA[:, d * T + off: d * T + off + n],
                                                     in_=tmp[hin][:, 0:n], func=AF.Identity,
                                                     scale=P(gnext + d)),
                         reads=[B_tmp[hin], B_prm], writes=[B_hn[g]])
                elif d < NHOLD:
                    S.op(act, lambda e: e.activation(out=hold_ap(d, g, off, n), in_=tmp[hin][:, 0:n],
                                                     func=AF.Identity, scale=P(gnext + d)),
                         reads=[B_tmp[hin], B_prm], writes=[B_hold[d][g]])
                    t_free(hin)
                else:
                    S.op(act, lambda e: e.activation(out=tmp[hin][:, 0:n], in_=tmp[hin][:, 0:n],
                                                     func=AF.Identity, scale=P(gnext + d)),
                         reads=[B_prm], writes=[B_tmp[hin]])
                    tail_tmps[g].append((d, hin))

            def finalize_chunks(g):
                off, n = GROUPS[g]
                bk = SQ[g]

                def c_rstd():
                    ti = t_alloc()
                    S.op(act, lambda e: e.activation(out=tmp[ti][:, 0:n], in_=banks[bk][:, 0:n], func=AF.Ln,
                                                     bias=epsb[:, 0:1], scale=1.0 / D),
                         reads=[B_bank[bk], B_const], writes=[B_tmp[ti]])
                    S.op(act, lambda e: e.activation(out=banks[bk][:, 0:n], in_=tmp[ti][:, 0:n], func=AF.Exp,
                                                     scale=-0.5),
                         reads=[B_tmp[ti]], writes=[B_bank[bk]])
                    t_free(ti)

                def c_rescale(k0, k1):
                    for k in range(k0, k1):
                        S.op(dve, lambda e, k=k: e.tensor_tensor(
                            out=bufA[:, k * T + off: k * T + off + n], in0=bufA[:, k * T + off: k * T + off + n],
                            in1=banks[bk][:, 0:n], op=ALU.mult),
                            reads=[B_bank[bk]], writes=[B_hn[g]])

                def c_tail():
                    for d, hin in tail_tmps[g]:
                        S.op(dve, lambda e, hin=hin: e.tensor_tensor(out=tmp[hin][:, 0:n], in0=tmp[hin][:, 0:n],
                                                                     in1=banks[bk][:, 0:n], op=ALU.mult),
                             reads=[B_bank[bk]], writes=[B_tmp[hin]])
                        bo = Buf(f"o2d_{g}_{d}")
                        B_out2.append(bo)
                        S.dma(pool, tmp_st_ds[hin],
                              lambda e, d=d, hin=hin: e.dma_start(out=out_ap(g, d, d + 1),
                                                                  in_=tmp[hin][:, 0:n]),
                              reads=[B_tmp[hin]], writes=[bo])
                        t_free(hin)
                    del tail_tmps[g][:]

                def c_holds(d0, d1):
                    for d in range(d0, d1):
                        S.op(dve, lambda e, d=d: e.tensor_tensor(out=hold_ap(d, g, off, n),
                                                                 in0=hold_ap(d, g, off, n),
                                                                 in1=banks[bk][:, 0:n], op=ALU.mult),
                             reads=[B_bank[bk]], writes=[B_hold[d][g]])

                def c_store_a():
                    bo = Buf(f"o2a_{g}")
                    B_out2.append(bo)
                    S.dma(pool, out_ds, lambda e: e.dma_start(
                        out=out_ap(g, 0, 8), in_=bufA_f32[:, 8 * off:8 * off + 8 * n], max_dma_last_dim=4096),
                        reads=[B_hold[d][g] for d in range(0, 8)], writes=[bo])

                def c_store_bc():
                    bo = Buf(f"o2b_{g}")
                    B_out2.append(bo)
                    S.dma(pool, out_ds, lambda e: e.dma_start(
                        out=out_ap(g, 8, 11), in_=lru_big[:, 3 * off:3 * off + 3 * n], max_dma_last_dim=4096),
                        reads=[B_hold[d][g] for d in range(8, 11)], writes=[bo])
                    bo = Buf(f"o2c_{g}")
                    B_out2.append(bo)
                    S.dma(pool, out_ds, lambda e: e.dma_start(
                        out=out_ap(g, 11, 12), in_=hold_ap(11, g, off, n)),
                        reads=[B_hold[11][g]], writes=[bo])

                if not last:
                    return [lambda: (c_rstd(), c_rescale(0, 4)), lambda: c_rescale(4, 8),
                            lambda: c_rescale(8, 12), lambda: c_rescale(12, 16)]
                return [lambda: (c_rstd(), c_tail()), lambda: c_holds(0, 4),
                        lambda: (c_holds(4, 8), c_store_a()), lambda: (c_holds(8, 12), c_store_bc())]

            for d in range(KT - TAILD):
                unit = take_units(1)[0]
                for g in range(NG):
                    item(d, g, unit)
                prefetch(wpos["n"] + RING)
            units = take_units(TAILD)
            chunks = []
            for g in range(NG):
                for j in range(TAILD):
                    item(KT - TAILD + j, g, units[j])
                    if chunks:
                        chunks.pop(0)()
                        if not chunks:
                            DL.append(SQ[g - 1])
                while pend:
                    flush_one()
                chunks = finalize_chunks(g)
            for c in chunks:
                c()
            prefetch(wpos["n"] + RING)
            rel_bank(*sorted(set(SQ + DL)))

        def phase0_stream_group(g):
            off, n = GROUPS[g]
            bk = new_bank()
            for d in range(KT):
                hin = new_tmp()
                S.dma(sp, tmp_ds[hin],
                      lambda e, d=d, hin=hin: e.dma_start(out=tmp[hin][:, 0:n],
                                                          in_=xT[d * 128:(d + 1) * 128, off:off + n]),
                      reads=[], writes=[B_tmp[hin]])
                sq = new_tmp()
                S.op(act, lambda e, hin=hin, sq=sq: e.activation(out=tmp[sq][:, 0:n].bitcast(BF16)[:, 0:n],
                                                                 in_=tmp[hin][:, 0:n], func=AF.Square),
                     reads=[B_tmp[hin]], writes=[B_tmp[sq]])
                S.op(pe, lambda e, d=d, sq=sq: e.matmul(banks[bk][:, 0:n], lhsT=ones[:, :],
                                                        rhs=tmp[sq][:, 0:n].bitcast(BF16)[:, 0:n],
                                                        start=(d == 0), stop=(d == KT - 1)),
                     reads=[B_tmp[sq], B_const], writes=[B_bank[bk]])
                S.op(act, lambda e, d=d, hin=hin: e.activation(out=bufA[:, d * T + off: d * T + off + n],
                                                               in_=tmp[hin][:, 0:n], func=AF.Identity,
                                                               scale=P(_pc(0, P_G) + d)),
                     reads=[B_tmp[hin], B_prm], writes=[B_hn[g]])
            ti = new_tmp()
            S.op(act, lambda e: e.activation(out=tmp[ti][:, 0:n], in_=banks[bk][:, 0:n], func=AF.Sqrt,
                                             bias=epsb[:, 0:1], scale=1.0 / D),
                 reads=[B_bank[bk], B_const], writes=[B_tmp[ti]])
            S.op(dve, lambda e: e.reciprocal(out=banks[bk][:, 0:n], in_=tmp[ti][:, 0:n]),
                 reads=[B_tmp[ti]], writes=[B_bank[bk]])
            for k in range(KT):
                S.op(dve, lambda e, k=k: e.tensor_tensor(
                    out=bufA[:, k * T + off: k * T + off + n], in0=bufA[:, k * T + off: k * T + off + n],
                    in1=banks[bk][:, 0:n], op=ALU.mult),
                    reads=[B_bank[bk]], writes=[B_hn[g]])
            rel_bank(bk)

        def first_pair(gcol0):
            uL = take_units(2)
            phase0_load(xT, [], 0, 0)
            prefetch(RING, extra_reads=[B_stage[0]])
            uC = take_units(4)
            phase0_load(xT, [B_stage[0]], 1, 1)
            phase0_compute(0, 0, gcol0, False)
            queue = []
            prev_c = None
            for g in range(NG):
                if prev_c is not None:
                    conv_stage(0, 0, prev_c[0], prev_c[1], 1)
                bL = lru_main(0, 0, g, uL)
                if g == NG - 1:
                    prefetch(wpos["n"] + 2)
                lru_stage1(0, 0, g, bL)
                if g + 1 < NG:
                    phase0_compute(g + 1, (g + 1) % 2, gcol0, False)
                if g == NG - 1:
                    lru_stage2(0, 0, g, lru_gates(0, 0, g))
                    scan_pass(0, 0, 0, 1)
                    exchange(0, 0)
                bC = conv_main(0, 0, g, uC)
                prev_c = (g, bC)
                if g < NG - 1:
                    lru_stage2(0, 0, g, lru_gates(0, 0, g))
                if g + 2 < NG:
                    phase0_load(xT, [], g + 2, g % 2)
            prefetch(wpos["n"] + RING)
            conv_stage(0, 0, prev_c[0], prev_c[1], 1)
            scan_pass(0, 0, 0, 2)

        dbg_bufs = []
        prefetch(2)
        for l in range(NL):
            S.dma(pool, wgs_ds, lambda e, l=l: e.dma_start(out=wgs[:, :], in_=wg[l * 128:(l + 1) * 128, :]),
                  writes=[B_wgs])
            if l == 0:
                src_ap, src_bufs = xT, [[] for _ in range(NG)]
            else:
                src_ap = spill[l - 1]
                src_bufs = [[B_spill[l - 1][d][g] for d in range(KT)] for g in range(NG)]
            gcol0 = _pc(l, P_G)
            first_slot = 0
            if l == 0:
                first_pair(gcol0)
                first_slot = 2
            pending2 = None
            deferred = []
            for slot in range(first_slot, 16):
                s = slot // 2
                ybase = slot
                if slot % 2 == 0:
                    units = take_units(2)
                    lag = 2
                    queue = []
                    for g in range(NG):
                        bks = lru_main(l, s, g, units)
                        lru_stage1(l, s, g, bks)
                        queue.append(g)
                        if len(queue) > lag:
                            pg = queue.pop(0)
                            lru_stage2(l, s, pg, lru_gates(l, s, pg))
                        if l == 0 and slot == 0 and g + 1 < NG:
                            if STREAM0:
                                phase0_stream_group(g + 1)
                            else:
                                phase0_group(src_ap, src_bufs[g + 1], g + 1, (g + 1) % 2, gcol0, False)
                    deferred = [(l, s, pg) for pg in queue]
                    prefetch(wpos["n"] + RING)
                    pending2 = (l, s, ybase)
                else:
                    units = take_units(4)
                    prev_item = None
                    for g in range(NG):
                        bks = conv_main(l, s, g, units)
                        if prev_item is not None:
                            conv_stage(l, s, prev_item[0], prev_item[1], ybase)
                        prev_item = (g, bks)
                        if deferred:
                            dl, ds_, dg = deferred.pop(0)
                            lru_stage2(dl, ds_, dg, lru_gates(dl, ds_, dg))
                            if not deferred and pending2 is not None:
                                scan_pass(pending2[0], pending2[1], pending2[2], 1)
                                exchange(pending2[0], pending2[1])
                        if g == 4 and pending2 is not None:
                            scan_pass(pending2[0], pending2[1], pending2[2], 2)
                            pending2 = None
                    conv_stage(l, s, prev_item[0], prev_item[1], ybase)
                    prefetch(wpos["n"] + RING)
            phase2_v2(l, src_ap, src_bufs)
        S.final_wait(sp, B_out + B_out2 + ([B_spill[l][d][g] for l in range(NL) for d in range(KT) for g in range(NG)]
                                    if dbg else []) + dbg_bufs)

        block = es.enter_context(nc.Block())

        @block.tensor
        def _(e):
            for f in pe.prog:
                f(e)

        @block.scalar
        def _(e):
            for f in act.prog:
                f(e)

        @block.vector
        def _(e):
            for f in dve.prog:
                f(e)

        @block.gpsimd
        def _(e):
            for f in pool.prog:
                f(e)

        @block.sync
        def _(e):
            for f in sp.prog:
                f(e)
    return nc


def _tile_units(w, col_blocks):
    out = []
    for c0 in col_blocks:
        blk = w[:, c0:c0 + 128]
        out.append(blk.reshape(KT, 128, 128).transpose(1, 0, 2).reshape(128, KT * 128))
    return out


def _prep_shared(inp):
    f = np.float32
    w_in = np.asarray(inp["w_in"], f)
    w_out = np.asarray(inp["w_out"], f)
    wi_units, wo_units, wg_l = [], [], []
    for l in range(NL):
        cols = []
        for s in range(8):
            cols += [s * 128, 1024 + s * 128]
            cols += [2048 + s * 128, 3072 + s * 128, 4096 + s * 128, 5120 + s * 128]
        wi_units += _tile_units(w_in[l], cols)
        rows = []
        for s in range(8):
            rows += list(range(s * 128, (s + 1) * 128))
            rows += list(range(1024 + s * 128, 1024 + (s + 1) * 128))
        wp = w_out[l][rows, :]
        wo_units += _tile_units(wp, [d * 128 for d in range(16)])
        g = np.zeros((128, 8, 2, 128), f)
        for s in range(8):
            for gi, nm in enumerate(["lru_wr", "lru_wi"]):
                wgt = np.asarray(inp[nm], f)[l]
                for hh in range(2):
                    g[hh * 64:(hh + 1) * 64, s, gi, hh * 64:(hh + 1) * 64] = wgt[2 * s + hh]
        wg_l.append(g.reshape(128, 8 * 2 * 128))
    prm = np.zeros((128, NPRM), f)

    def pp(v, n):
        return np.asarray(v, f).reshape(n, 128).T

    for l in range(NL):
        prm[:, _pc(l, P_G):_pc(l, P_G) + 16] = pp(inp["norm_g"][l], 16)
        caw = np.asarray(inp["conv_a_w"], f)[l]
        for s in range(8):
            for k in range(4):
                prm[:, _pc(l, P_CAW) + s * 4 + k] = caw[k, s * 128:(s + 1) * 128]
        prm[:, _pc(l, P_CAB):_pc(l, P_CAB) + 8] = pp(inp["conv_a_b"][l], 8)
        prm[:, _pc(l, P_BR):_pc(l, P_BR) + 8] = pp(inp["lru_br"][l], 8)
        prm[:, _pc(l, P_BI):_pc(l, P_BI) + 8] = pp(inp["lru_bi"][l], 8)
        prm[:, _pc(l, P_LAM):_pc(l, P_LAM) + 8] = pp(inp["lru_lambda"][l], 8)
        cbw = np.asarray(inp["conv_b_w"], f)[l]
        for s in range(8):
            for k in range(3):
                prm[:, _pc(l, P_CBW) + s * 3 + k] = cbw[k, s * 128:(s + 1) * 128]
    prm[:, P_FG:P_FG + 16] = pp(inp["final_g"], 16)
    return (np.ascontiguousarray(np.concatenate(wi_units, 0)),
            np.ascontiguousarray(np.concatenate(wo_units, 0)),
            np.ascontiguousarray(np.concatenate(wg_l, 0)), prm)


def _make_in_maps(inp):
    f = np.float32
    x = np.asarray(inp["x"], f)
    meta = np.asarray(inp["meta"], f)
    wi, wo, wgm, prm = _prep_shared(inp)
    in_maps = []
    for c in range(8):
        b, half = c // 2, c % 2
        loc = np.zeros((T, D), f)
        tm = np.ones((128, 8), f)
        p = prm.copy()
        if half == 0:
            loc[NPRE:NPRE + NMETA] = meta
            loc[NPRE + NMETA:] = x[b, :SPLIT]
            tm[:, :NPRE] = 0.0
            p[:, P_FLAG] = 0.0
        else:
            loc[:] = x[b, SPLIT - NPRE:]
            p[:, P_FLAG] = 1.0
        in_maps.append({"xT": np.ascontiguousarray(loc.T), "w_in": wi, "w_out": wo, "wg": wgm,
                        "prm": p, "tmask": tm})
    return in_maps


def kernel(**inputs):
    in_maps = _make_in_maps(inputs)
    nc = build_program()
    res = run_bass_kernel_spmd(nc, in_maps, core_ids=list(range(8)))
    out = np.empty((4, SEQ, D), np.float32)
    for c in range(8):
        b, half = c // 2, c % 2
        og = res.results[c]["outT"]
        o = np.empty((D, T), np.float32)
        for off, n in GROUPS:
            blk = og[:, KT * off:KT * (off + n)].reshape(128, KT, n)
            o.reshape(KT, 128, T)[:, :, off:off + n] = blk.transpose(1, 0, 2)
        if half == 0:
            out[b, :SPLIT] = o[:, NPRE + NMETA:].T
        else:
            out[b, SPLIT:] = o[:, NPRE:].T
    return out
```
